# Optimizing a Trainium2 kernel written in Bass

```python
import jax, jax.numpy as jnp
from jax import lax
import numpy as np

D_MODEL = 1024
BATCH = 2
SEQ = 8192
DEPTH = 1

N_META = 16
D_MIX = D_MODEL
GLA_HEADS = 4
GLA_DK = D_MIX // 4 // GLA_HEADS
GLA_DV = D_MIX // 2 // GLA_HEADS
GLA_RANK = 16
GLA_TAU = 16.0
GLA_CHUNK = 64
GLA_PAD = GLA_CHUNK - N_META
SWA_HEADS = 8
SWA_KV_HEADS = 2
SWA_GROUP = SWA_HEADS // SWA_KV_HEADS
SWA_HD = D_MIX // 2 // SWA_HEADS
SWA_WINDOW = 128
SWA_BLOCK = 128
ROPE_DIM = SWA_HD // 4
ROPE_THETA = 500000.0
D_FF = 4 * D_MODEL
EPS = 1e-5

IN_SIZES = (GLA_HEADS * GLA_DK,
            GLA_HEADS * GLA_DK,
            GLA_HEADS * GLA_DV,
            GLA_HEADS * GLA_DV,
            GLA_RANK,
            SWA_HEADS * SWA_HD,
            SWA_KV_HEADS * SWA_HD,
            SWA_KV_HEADS * SWA_HD)
D_IN = sum(IN_SIZES)

kernel_name = "hybrid_gla_swa_sink_meta_layer"


def rmsnorm(x, w):
    xf = x.astype(jnp.float32)
    y = xf * lax.rsqrt(jnp.mean(jnp.square(xf), axis=-1, keepdims=True) + EPS)
    return (y * w.astype(jnp.float32)).astype(x.dtype)


def partial_rope(x, pos):
    inv_freq = 1.0 / (ROPE_THETA ** (jnp.arange(0, ROPE_DIM, 2, dtype=jnp.float32) / ROPE_DIM))
    ang = pos.astype(jnp.float32)[:, None] * inv_freq[None, :]
    ang = jnp.concatenate([ang, ang], axis=-1)[:, None, :]
    cos, sin = jnp.cos(ang), jnp.sin(ang)
    xr = x[..., :ROPE_DIM].astype(jnp.float32)
    half = ROPE_DIM // 2
    rot = jnp.concatenate([-xr[..., half:], xr[..., :half]], axis=-1)
    xr = (xr * cos + rot * sin).astype(x.dtype)
    return jnp.concatenate([xr, x[..., ROPE_DIM:]], axis=-1)


def gla_chunk_step(S, inp):
    q, k, v, g = inp
    b = jnp.cumsum(g, axis=2)
    causal = jnp.tril(jnp.ones((GLA_CHUNK, GLA_CHUNK), dtype=bool))
    diff = b[:, :, :, None, :] - b[:, :, None, :, :]
    decay = jnp.exp(jnp.where(causal[None, None, :, :, None], diff, -jnp.inf))
    A = jnp.einsum('bhid,bhjd,bhijd->bhij', q, k, decay)
    o = (jnp.einsum('bhij,bhjv->bhiv', A, v)
         + jnp.einsum('bhid,bhdv->bhiv', q * jnp.exp(b), S))
    b_last = b[:, :, -1:, :]
    S = (jnp.exp(b_last[:, :, 0, :])[..., None] * S
         + jnp.einsum('bhjd,bhjv->bhdv', k * jnp.exp(b_last - b), v))
    return S, o


def gla_mixer(q, k, v, r, lr, w_gate_up, b_gate, gla_norm_w):
    B, L, _ = q.shape
    dtype = q.dtype
    g = jax.nn.log_sigmoid((lr @ w_gate_up + b_gate).astype(jnp.float32)) / GLA_TAU
    q = q.astype(jnp.float32).reshape(B, L, GLA_HEADS, GLA_DK) * (GLA_DK ** -0.5)
    k = k.astype(jnp.float32).reshape(B, L, GLA_HEADS, GLA_DK)
    v = v.astype(jnp.float32).reshape(B, L, GLA_HEADS, GLA_DV)
    g = g.reshape(B, L, GLA_HEADS, GLA_DK)
    pad = ((0, 0), (GLA_PAD, 0), (0, 0), (0, 0))
    Lp = L + GLA_PAD
    n_chunks = Lp // GLA_CHUNK

    def to_chunks(t):
        t = jnp.pad(t, pad).reshape(B, n_chunks, GLA_CHUNK, GLA_HEADS, t.shape[-1])
        return t.transpose(1, 0, 3, 2, 4)

    S0 = jnp.zeros((B, GLA_HEADS, GLA_DK, GLA_DV), jnp.float32)
    _, o = lax.scan(gla_chunk_step, S0, (to_chunks(q), to_chunks(k), to_chunks(v), to_chunks(g)))
    o = o.transpose(1, 0, 3, 2, 4).reshape(B, Lp, GLA_HEADS, GLA_DV)[:, GLA_PAD:]
    o = rmsnorm(o.astype(dtype), gla_norm_w)
    o = o * jax.nn.silu(r.reshape(B, L, GLA_HEADS, GLA_DV))
    return o.reshape(B, L, GLA_HEADS * GLA_DV)


def sink_softmax(scores, sink):
    sink_b = jnp.broadcast_to(sink, scores.shape[:-1] + (1,))
    p = jax.nn.softmax(jnp.concatenate([scores, sink_b], axis=-1), axis=-1)
    return p[..., :-1]


def swa_mixer(q, k, v, sinks, pos):
    B, L, _ = q.shape
    dtype = q.dtype
    q = partial_rope(q.reshape(B, L, SWA_HEADS, SWA_HD), pos) * (SWA_HD ** -0.5)
    k = partial_rope(k.reshape(B, L, SWA_KV_HEADS, SWA_HD), pos)
    v = v.reshape(B, L, SWA_KV_HEADS, SWA_HD)
    q = q.reshape(B, L, SWA_KV_HEADS, SWA_GROUP, SWA_HD)
    qm, qr = q[:, :N_META], q[:, N_META:]
    km, kr = k[:, :N_META], k[:, N_META:]
    vm, vr = v[:, :N_META], v[:, N_META:]
    sink = sinks.astype(jnp.float32).reshape(SWA_KV_HEADS, SWA_GROUP)

    sm = jnp.einsum('bqkgd,bjkd->bkgqj', qm, km).astype(jnp.float32)
    mmask = jnp.tril(jnp.ones((N_META, N_META), dtype=bool))
    pm = sink_softmax(jnp.where(mmask, sm, -jnp.inf), sink[None, :, :, None, None])
    om = jnp.einsum('bkgqj,bjkd->bqkgd', pm.astype(dtype), vm).reshape(B, N_META, SWA_HEADS * SWA_HD)

    S = L - N_META
    nb = S // SWA_BLOCK
    qb = qr.reshape(B, nb, SWA_BLOCK, SWA_KV_HEADS, SWA_GROUP, SWA_HD)

    def band(t):
        cur = t.reshape(B, nb, SWA_BLOCK, SWA_KV_HEADS, SWA_HD)
        prev = jnp.pad(t, ((0, 0), (SWA_BLOCK, 0), (0, 0), (0, 0)))[:, :S]
        prev = prev.reshape(B, nb, SWA_BLOCK, SWA_KV_HEADS, SWA_HD)
        return jnp.concatenate([prev, cur], axis=2)

    kw, vw = band(kr), band(vr)
    s_meta = jnp.einsum('bnqkgd,bjkd->bnkgqj', qb, km).astype(jnp.float32)
    s_win = jnp.einsum('bnqkgd,bnjkd->bnkgqj', qb, kw).astype(jnp.float32)
    rr = jnp.arange(SWA_BLOCK)[:, None]
    jj = jnp.arange(2 * SWA_BLOCK)[None, :]
    dist = SWA_BLOCK + rr - jj
    in_band = (dist >= 0) & (dist < SWA_WINDOW)
    blk = jnp.arange(nb)[:, None, None]
    wmask = in_band[None] & ((blk > 0) | (jj[None] >= SWA_BLOCK))
    s_win = jnp.where(wmask[None, :, None, None], s_win, -jnp.inf)
    p = sink_softmax(jnp.concatenate([s_meta, s_win], axis=-1),
                     sink[None, None, :, :, None, None]).astype(dtype)
    orr = (jnp.einsum('bnkgqj,bjkd->bnqkgd', p[..., :N_META], vm)
           + jnp.einsum('bnkgqj,bnjkd->bnqkgd', p[..., N_META:], vw))
    orr = orr.reshape(B, S, SWA_HEADS * SWA_HD)
    return jnp.concatenate([om, orr], axis=1)


def hybrid_layer(h, pos, norm_mix_w, w_in, w_gate_up, b_gate, gla_norm_w, sinks,
                 w_out, norm_ff_w, w_ff1, w_ff2):
    u = rmsnorm(h, norm_mix_w)
    proj = u @ w_in
    split_points = np.cumsum(IN_SIZES)[:-1].tolist()
    gq, gk, gv, gr, glr, sq, sk, sv = jnp.split(proj, split_points, axis=-1)
    o_gla = gla_mixer(gq, gk, gv, gr, glr, w_gate_up, b_gate, gla_norm_w)
    o_swa = swa_mixer(sq, sk, sv, sinks, pos)
    h = h + jnp.concatenate([o_gla, o_swa], axis=-1) @ w_out
    f = rmsnorm(h, norm_ff_w)
    return h + jnp.square(jax.nn.relu(f @ w_ff1)) @ w_ff2


def setup_inputs(seed: int = 0) -> dict:
    key = jax.random.key(seed)
    ks = jax.random.split(key, 14)
    f32 = jnp.float32
    nrm = lambda k, shape, s: jax.random.normal(k, shape, f32) * s
    return {
        "x": nrm(ks[0], (BATCH, SEQ, D_MODEL), 1.0),
        "meta_tokens": nrm(ks[1], (N_META, D_MODEL), 1.0),
        "norm_mix_w": 1.0 + nrm(ks[2], (DEPTH, D_MODEL), 0.02),
        "w_in": nrm(ks[3], (DEPTH, D_MODEL, D_IN), D_MODEL ** -0.5),
        "w_gate_up": nrm(ks[4], (DEPTH, GLA_RANK, GLA_HEADS * GLA_DK), GLA_RANK ** -0.5),
        "b_gate": nrm(ks[5], (DEPTH, GLA_HEADS * GLA_DK), 0.1),
        "gla_norm_w": 1.0 + nrm(ks[6], (DEPTH, GLA_DV), 0.02),
        "sinks": nrm(ks[7], (DEPTH, SWA_HEADS), 1.0),
        "w_out": nrm(ks[8], (DEPTH, D_MIX, D_MODEL), D_MIX ** -0.5),
        "norm_ff_w": 1.0 + nrm(ks[9], (DEPTH, D_MODEL), 0.02),
        "w_ff1": nrm(ks[10], (DEPTH, D_MODEL, D_FF), D_MODEL ** -0.5),
        "w_ff2": nrm(ks[11], (DEPTH, D_FF, D_MODEL), D_FF ** -0.5),
        "final_norm_w": 1.0 + nrm(ks[12], (D_MODEL,), 0.02),
    }


def reference(x, meta_tokens, norm_mix_w, w_in, w_gate_up, b_gate, gla_norm_w, sinks,
              w_out, norm_ff_w, w_ff1, w_ff2, final_norm_w):
    B = x.shape[0]
    meta = jnp.broadcast_to(meta_tokens[None].astype(x.dtype), (B, N_META, D_MODEL))
    h = jnp.concatenate([meta, x], axis=1)
    pos = jnp.arange(h.shape[1], dtype=jnp.int32)
    for layer in range(DEPTH):
        h = hybrid_layer(h, pos, norm_mix_w[layer], w_in[layer], w_gate_up[layer], b_gate[layer],
                         gla_norm_w[layer], sinks[layer], w_out[layer], norm_ff_w[layer],
                         w_ff1[layer], w_ff2[layer])
    return rmsnorm(h, final_norm_w)[:, N_META:]
```

```python
import numpy as np
from contextlib import ExitStack

import concourse.bass as bass
import concourse.mybir as mybir
from concourse.bass_utils import run_bass_kernel_spmd

F32 = mybir.dt.float32
BF16 = mybir.dt.bfloat16
AF = mybir.ActivationFunctionType
ALU = mybir.AluOpType

NCORES = 8
D = 1024
SEQ = 8192
BATCH = 2
NSEG = 4
TOK = SEQ // NSEG
NT = TOK // 128
GT = 2
NG = NT // GT
GW = GT * 128
N_META = 16
DFF = 4096
FB = 1024
NFB = DFF // FB
EPS = 1e-5
NEG = -30000.0
ROPE_THETA = 500000.0
DEBUG_STOP = None
PENG = 'dve'
VARX = False
PIPELINE = True

C_GQ, C_GK, C_GR, C_SQ, C_SK, C_GV, C_SV, C_LR = 0, 256, 512, 1024, 1536, 1664, 2176, 2304
NCOL = 2320
CF_RESET, CF_FLAGS, CF_NMW, CF_NFW, CF_GNW, CF_BGN = 0, 256, 264, 272, 280, 281
CF_ONE, CF_EIGHTH = 283, 284
CF_N = 285


SYNC_LAT = 0.3
SPLIT_BACK = False
LIST_BIAS = 0.0
TABLE_PENALTY = 0.0
PSUM_KEYS = {'pT', 'F0', 'F1', 'F2', 'SC0', 'SC1', 'pO0', 'pO1', 'pD0', 'pD1'}


class Prog:
    def __init__(self, nc, stack):
        self.nc = nc
        self.stack = stack
        self.ops = []
        self.last_w = {}
        self.readers = {}
        self.last_on = {}
        self.dsem = {}
        self.esem = {}
        self._cap = None
        self.fin = []
        self.eng_free = {}
        self.act_table = None

    def _deps_of(self, o):
        d = set()
        for r in o['reads']:
            j = self.last_w.get(r)
            if j is not None:
                d.add(j)
        for w in o['writes']:
            j = self.last_w.get(w)
            if j is not None:
                d.add(j)
            d.update(self.readers.get(w, ()))
        d.update(o['extra'])
        return d

    def est_start(self, o):
        t = self.eng_free.get(o['eng'], 0.0)
        for j in self._deps_of(o):
            lat = 0.03 if (self.ops[j]['eng'] == o['eng'] and self.ops[j]['kind'] == 'c') else SYNC_LAT
            t = max(t, self.fin[j] + lat)
        tb = o.get('table')
        if tb is not None and tb != self.act_table:
            t += 1.3 + TABLE_PENALTY
        return t

    def play_sched(self, lists):
        lists = [l for l in lists if l]
        idx = [0] * len(lists)
        while True:
            best, bt = None, None
            for k, l in enumerate(lists):
                if idx[k] < len(l):
                    t = self.est_start(l[idx[k]]) + k * LIST_BIAS
                    if best is None or t < bt - 1e-9:
                        best, bt = k, t
            if best is None:
                break
            self._add(lists[best][idx[best]])
            idx[best] += 1

    def capture(self):
        self._cap = []

    def end_capture(self):
        c, self._cap = self._cap, None
        return c

    def play(self, ops):
        for o in ops:
            self._add(o)

    def _add(self, o):
        if self._cap is not None:
            self._cap.append(o)
            return None
        i = len(self.ops)
        raw, oth = set(), set()
        for r in o['reads']:
            j = self.last_w.get(r)
            if j is not None:
                raw.add(j)
        for w in o['writes']:
            j = self.last_w.get(w)
            if j is not None:
                oth.add(j)
            for rr in self.readers.get(w, ()):
                oth.add(rr)
        for e in o['extra']:
            raw.add(e)
        oth -= raw
        o['raw'], o['oth'] = raw, oth
        o['prod'] = {r: self.ops[self.last_w[r]].get('label') for r in o['reads'] if r in self.last_w}
        for r in o['reads']:
            self.readers.setdefault(r, []).append(i)
        for r in o['reads']:
            if r in PSUM_KEYS and r not in o['writes']:
                self.last_w[r] = i
                self.readers[r] = []
        for w in o['writes']:
            self.last_w[w] = i
            self.readers[w] = []
        t0 = self.eng_free.get(o['eng'], 0.0)
        for j in raw | oth:
            lat = 0.03 if (self.ops[j]['eng'] == o['eng'] and self.ops[j]['kind'] == 'c') else SYNC_LAT
            t0 = max(t0, self.fin[j] + lat)
        dur = o.get('dur', 0.3)
        tb = o.get('table')
        if tb is not None and tb != self.act_table:
            t0 += 1.3
            self.act_table = tb
        if o['kind'] == 'd':
            self.eng_free[o['eng']] = t0 + 0.15
        else:
            self.eng_free[o['eng']] = t0 + dur
        self.fin.append(t0 + dur)
        self.ops.append(o)
        self.last_on[o['eng']] = i
        if any(r.startswith('win_') for r in o['reads']):
            self.last_win = i
        return i

    def op(self, eng, fn, reads=(), writes=(), extra=(), cost=0, dur=None, table=None):
        if dur is None:
            dur = cost / 1950.0 if eng == 'pe' else 0.3
        return self._add(dict(eng=eng, fn=fn, reads=list(reads), writes=list(writes), kind='c',
                              extra=list(extra), cost=cost, dur=dur, table=table))

    def dma(self, eng, fn, sem, reads=(), writes=(), extra=(), inc=16, dur=4.0):
        if sem not in self.dsem:
            self.dsem[sem] = self.stack.enter_context(self.nc.semaphore("d_" + sem))
        return self._add(dict(eng=eng, fn=fn, reads=list(reads), writes=list(writes), kind='d',
                              sem=sem, inc=inc, extra=list(extra), dur=dur))

    def emit(self):
        nc, ops = self.nc, self.ops
        engs = ['pe', 'act', 'dve', 'pool', 'sp']
        for e in engs:
            self.esem[e] = self.stack.enter_context(nc.semaphore("e_" + e))
        need = [False] * len(ops)
        for i, o in enumerate(ops):
            E = o['eng']
            wl = []
            for j in sorted(o['raw'] | o['oth']):
                pj = ops[j]
                if pj['kind'] == 'd':
                    wl.append(j)
                elif pj['eng'] == E:
                    if o['kind'] == 'd' or E in ('act', 'dve', 'pool'):
                        wl.append(j)
                else:
                    wl.append(j)
            o['wl'] = wl
            for j in wl:
                if ops[j]['kind'] == 'c':
                    need[j] = True
        cnt = {}
        cum = {}
        for i, o in enumerate(ops):
            if o['kind'] == 'c':
                if need[i]:
                    cnt[o['eng']] = cnt.get(o['eng'], 0) + 1
                    o['ms'] = cnt[o['eng']]
            else:
                o['cb'] = cum.get(o['sem'], 0)
                cum[o['sem']] = o['cb'] + o['inc']
                o['cum'] = cum[o['sem']]

        def emit_engine(E, eng):
            seen = {}
            for i, o in enumerate(ops):
                if o['eng'] != E:
                    continue
                tgt = {}
                for j in o['wl']:
                    pj = ops[j]
                    if pj['kind'] == 'd':
                        k, v, s = ('d', pj['sem']), pj['cum'], self.dsem[pj['sem']]
                    else:
                        k, v, s = ('c', pj['eng']), pj['ms'], self.esem[pj['eng']]
                    if v > tgt.get(k, (0, None))[0]:
                        tgt[k] = (v, s)
                if o['kind'] == 'd' and o['cb'] > 0:
                    k = ('d', o['sem'])
                    if o['cb'] > tgt.get(k, (0, None))[0]:
                        tgt[k] = (o['cb'], self.dsem[o['sem']])
                for k, (v, s) in tgt.items():
                    if v > seen.get(k, 0):
                        eng.wait_ge(s, v)
                        seen[k] = v
                if o['fn'] is None:
                    continue
                ins = o['fn'](eng)
                if o['kind'] == 'd':
                    ins.then_inc(self.dsem[o['sem']], o['inc'])
                elif need[i]:
                    ins.then_inc(self.esem[E], 1)

        with nc.Block() as block:
            @block.tensor
            def _(e):
                emit_engine('pe', e)

            @block.scalar
            def _(e):
                emit_engine('act', e)

            @block.vector
            def _(e):
                emit_engine('dve', e)

            @block.gpsimd
            def _(e):
                emit_engine('pool', e)

            @block.sync
            def _(e):
                emit_engine('sp', e)


def build_program():
    nc = bass.Bass("TRN2", target_bir_lowering=False, dynamic_dma_scratch_size=8192)
    st = ExitStack()
    P = Prog(nc, st)

    def dram(name, shape, dt=F32, kind="ExternalInput"):
        return nc.dram_tensor(name, shape, dt, kind=kind).ap()

    def sb(name, shape, dt):
        t = st.enter_context(nc.sbuf_tensor(name, shape, dt))
        return t[tuple(slice(None) for _ in shape)]

    def psum(name, shape, dt):
        t = st.enter_context(nc.psum_tensor(name, shape, dt))
        return t[tuple(slice(None) for _ in shape)]

    xo_d = dram("xo", [TOK, D])
    xh_d = dram("xh", [128, D])
    xm_d = dram("xm", [128, D])
    win_d = dram("w_in", [D, NCOL])
    wout_d = dram("w_out", [D, D])
    w1_d = dram("w_ff1", [D, DFF])
    w2_d = dram("w_ff2", [DFF, D])
    wgu_d = dram("wgu", [16, 256])
    cstb_d = dram("cstb", [128, 7 * 128])
    maskm_d = dram("maskm", [128, 1024])
    cstf_d = dram("cstf", [128, CF_N])
    finw_d = dram("finw", [128, D])
    sinks_d = dram("sinks", [1, 8])
    cos_d = dram("cosT", [128, 2304])
    sin_d = dram("sinT", [128, 2304])
    out_d = dram("out", [TOK, D], kind="ExternalOutput")
    cin_d = dram("cc_in", [128, 514], kind="Internal")
    cout_d = dram("cc_out", [4 * 128, 514], kind="Internal")

    h = sb("h", [128, NT, D], F32)
    cstb = sb("cstb_s", [128, 7, 128], BF16)
    ident, maskP, maskC, maskH, causal, onesdiv, Pm = [cstb[:, i, :] for i in range(7)]
    maskM = sb("maskM", [128, 2, 512], BF16)
    cstf = sb("cstf_s", [128, CF_N], F32)
    resetm = cstf[:, CF_RESET:CF_RESET + 256]
    flags = cstf[:, CF_FLAGS:CF_FLAGS + 8]
    nmw = cstf[:, CF_NMW:CF_NMW + 8]
    nfw = cstf[:, CF_NFW:CF_NFW + 8]
    gnw = cstf[:, CF_GNW:CF_GNW + 1]
    bgn = cstf[:, CF_BGN:CF_BGN + 2]
    c_one = cstf[:, CF_ONE:CF_ONE + 1]
    c_eighth = cstf[:, CF_EIGHTH:CF_EIGHTH + 1]
    wgu32 = sb("wgu_s", [16, 256], F32)
    wgu = sb("wgu_b", [16, 256], BF16)
    sinks_s = sb("sinks_s", [1, 8], F32)
    ones64 = sb("ones64", [128, 64], BF16)
    epst = sb("epst", [128, 1], F32)
    stat = sb("stat", [128, 16], F32)

    ARENA = 143 * 1024
    arena = sb("arena", [128, ARENA // 4], F32)
    apos = [0]

    def carve(shape, dt):
        n = int(np.prod(shape[1:]))
        nbytes = n * (4 if dt == F32 else 2)
        nbytes = (nbytes + 31) // 32 * 32
        off = apos[0]
        apos[0] += nbytes
        assert apos[0] <= ARENA, ("arena overflow", apos[0])
        base = arena[0:shape[0], off // 4:(off + nbytes) // 4]
        if dt != F32:
            base = base.bitcast(dt)
        ap = base[:, 0:n]
        if len(shape) == 3:
            ap = ap.rearrange("p (a b) -> p a b", a=shape[1])
        return ap

    win = carve([128, 8, NCOL], BF16)
    wout = carve([128, 8, D], BF16)
    tabc = [carve([128, GW], F32) for _ in range(2)]
    tabs = [carve([128, GW], F32) for _ in range(2)]
    utm = [carve([128, D], BF16) for _ in range(2)]
    uT = [carve([128, 8, GW], BF16) for _ in range(2)]
    xa = carve([128, D], F32)
    lrT = carve([16, GW], BF16)
    esp = carve([128, 2, GW], F32)
    csp = carve([128, 2, GW], F32)
    E1 = carve([128, 2, GW], F32)
    E2 = carve([128, 2, GW], F32)
    e1l = [carve([128, 2, GT], F32) for _ in range(2)]
    qf = carve([128, 2, GW], F32)
    kf = carve([128, 2, GW], F32)
    qtl = [carve([128, 2, GW], BF16) for _ in range(2)]
    ktl = [carve([128, 2, GW], BF16) for _ in range(2)]
    khT = carve([128, 2, GW], BF16)
    khtm = [carve([128, GT, 256], BF16) for _ in range(2)]
    Vg = [carve([128, GT, 512], BF16) for _ in range(2)]
    sg = [carve([128, 4, GW], BF16) for _ in range(2)]
    qr = [carve([128, 4, GW], BF16) for _ in range(2)]
    xsb = carve([128, GW], BF16)
    t1 = carve([128, GW], F32)
    t2 = carve([128, GW], F32)
    kS = carve([128, NT + 2, 128], BF16)
    VS = carve([128, NT + 2, 128], BF16)
    un0 = apos[0]
    PT = [carve([128, 3, 512], BF16) for _ in range(2)]
    rden = carve([128, 512], F32)
    un1 = apos[0]
    apos[0] = un0
    gath = carve([128, 3, 514], F32)
    assert apos[0] <= un1
    apos[0] = un1
    AT = carve([128, 512], BF16)
    S = carve([128, 2, 256], F32)
    Sb = carve([128, 2, 256], BF16)
    sq = carve([128, 512], BF16)
    rs = carve([128, 512], F32)
    on = carve([128, 512], F32)
    Sm = on[:, :].rearrange("p (a b) -> p a b", a=2)
    mixT = [carve([128, 8, 128], BF16) for _ in range(2)]
    clsum = carve([128, 2], F32)
    pay = carve([128, 514], F32)
    a1 = carve([128, 2], F32)
    tg = rs[:, :].rearrange("p (a b) -> p a b", a=2)
    ab_end = apos[0]

    apos[0] = 0
    W1b0 = carve([128, 8, FB], BF16)
    W2b0 = carve([128, 8, FB], BF16)
    assert apos[0] <= 8 * NCOL * 2
    fT = carve([128, 8, TOK], BF16)
    W1b = [W1b0, carve([128, 8, FB], BF16)]
    W2b = [W2b0, carve([128, 8, FB], BF16)]
    h1T = [carve([128, 8, 512], BF16) for _ in range(2)]
    rtmp = [carve([128, 512], F32) for _ in range(2)]
    utmC = [carve([128, D], BF16) for _ in range(2)]
    finw = carve([128, D], F32)
    yout = [carve([128, D], F32) for _ in range(2)]
    junkD = [carve([128, D], BF16) for _ in range(2)]

    pT = psum("pT", [128, 1024], BF16)
    pF = [psum(f"pF{i}", [128, 512], F32) for i in range(2)]
    pSC = [psum(f"pSC{i}", [128, 512], F32) for i in range(2)]
    pO = psum("pO", [128, 512], F32)
    pD = psum("pD", [128, 512], F32)
    pW = psum("pW", [128, 512], F32)

    fctr = [0]
    fmode = ['A']

    FBANK = [pF[0], pF[1], pW]

    def nextF():
        i = fctr[0] % (2 if (SPLIT_BACK and fmode[0] == 'B') else 3)
        fctr[0] += 1
        return i

    def Fslot(i, parts=128, width=GW):
        return FBANK[i][0:parts, 0:width]

    sctr = [0]

    def nextSC():
        i = sctr[0] % 2
        sctr[0] += 1
        return i

    def mmchain(out, pairs, reads, writes, start=True, stop=True):
        def fn(e):
            n = len(pairs)
            ins = None
            for idx, (l, r) in enumerate(pairs):
                ins = e.matmul(out=out, lhsT=l, rhs=r, start=(start and idx == 0), stop=(stop and idx == n - 1))
            return ins
        mult = 4 if pairs[0][1].dtype == F32 else 1
        P.op('pe', fn, reads, writes, cost=mult * sum(max(r.free_size(), 64) for (_l, r) in pairs))

    def mmlist(items, reads, writes):
        def fn(e):
            ins = None
            for (o, l, r, s, t) in items:
                ins = e.matmul(out=o, lhsT=l, rhs=r, start=s, stop=t)
            return ins
        P.op('pe', fn, reads, writes, cost=sum(max(it[2].free_size(), 64) for it in items))

    def transposes(items, reads, writes):
        def fn(e):
            ins = None
            for (o, i_) in items:
                ins = e.transpose(out=o, in_=i_, identity=ident)
            return ins
        P.op('pe', fn, list(reads) + ['cstb'], writes, cost=128 * len(items))

    def act(out, in_, func, reads, writes, bias=None, scale=None, accum=None):
        kw = {}
        if bias is not None:
            kw['bias'] = bias
        if scale is not None:
            kw['scale'] = scale
        if accum is not None:
            kw['accum_out'] = accum
        table = 'silu' if func == AF.Silu else ('lnexp' if func in (AF.Exp, AF.Ln) else None)
        P.op('act', lambda e: e.activation(out=out, in_=in_, func=func, **kw), reads, writes,
             dur=0.22 + out.free_size() / 1400.0 + (0.1 if accum is not None else 0.0), table=table)

    def tt(eng, out, in0, in1, op, reads, writes):
        P.op(eng, lambda e: e.tensor_tensor(out=out, in0=in0, in1=in1, op=op), reads, writes,
             dur=0.07 + out.free_size() / 960.0)

    def ts(eng, out, in0, s1, s2, op0, op1, reads, writes):
        if op1 is None:
            P.op(eng, lambda e: e.tensor_scalar(out=out, in0=in0, scalar1=s1, scalar2=None, op0=op0), reads, writes,
                 dur=0.07 + out.free_size() / 1300.0)
        else:
            P.op(eng, lambda e: e.tensor_scalar(out=out, in0=in0, scalar1=s1, scalar2=s2, op0=op0, op1=op1),
                 reads, writes, dur=0.07 + out.free_size() / 1300.0)

    def stt(out, in0, scalar, in1, op0, op1, reads, writes):
        P.op('dve', lambda e: e.scalar_tensor_tensor(out=out, in0=in0, scalar=scalar, in1=in1, op0=op0, op1=op1),
             reads, writes, dur=0.07 + out.free_size() / 960.0)

    def dma(eng, out, in_, sem, reads, writes, extra=(), **kw):
        return P.dma(eng, lambda e: e.dma_start(out=out, in_=in_, **kw), sem, reads, writes, extra=extra)

    dma('sp', cstf, cstf_d, 'c0', [], ['cstf'])
    dma('sp', wgu32, wgu_d, 'c1', [], ['wgu32'])
    dma('sp', sinks_s, sinks_d, 'c2', [], ['sinks'])
    dma('pool', cstb, cstb_d.rearrange("p (a b) -> p a b", a=7), 'c3', [], ['cstb'])
    dma('pool', maskM, maskm_d.rearrange("p (a b) -> p a b", a=2), 'c4', [], ['maskM'])
    win_v = win_d.rearrange("(k p) c -> p k c", p=128)
    dma('pool', win[:, :, C_GK:C_GR], win_v[:, :, C_GK:C_GR], 'w0', [], ['win_gk'])
    i_w1 = dma('pool', win[:, :, C_GV:NCOL], win_v[:, :, C_GV:NCOL], 'w1', [], ['win_tm'])
    dma('sp', xa, xm_d, 'xa0', [], ['xa'])
    i_x = None
    for t in range(NT):
        i_x = dma('sp', h[:, t, :], xo_d[t * 128:(t + 1) * 128, :], f'x{t % 4}', [], [f'h{t}'],
                  extra=[i_w1] if t >= 2 else [])
    dma('pool', win[:, :, C_GQ:C_GK], win_v[:, :, C_GQ:C_GK], 'w2', [], ['win_gq'], extra=[i_x])
    dma('pool', win[:, :, C_GR:C_SQ], win_v[:, :, C_GR:C_SQ], 'w3', [], ['win_gr'], extra=[i_x])
    dma('pool', win[:, :, C_SQ:C_GV], win_v[:, :, C_SQ:C_GV], 'w4', [], ['win_s'], extra=[i_x])
    dma('pool', wout, wout_d.rearrange("(k p) c -> p k c", p=128), 'w5', [], ['wout'], extra=[i_x])
    WIN_ALL = ['win_gk', 'win_tm', 'win_gq', 'win_gr', 'win_s']

    ts('dve', bgn, bgn, -1.0, None, ALU.mult, None, ['cstf'], ['cstf'])
    P.op('dve', lambda e: e.tensor_copy(out=wgu, in_=wgu32), ['wgu32'], ['wgu'])
    P.op('dve', lambda e: e.memset(ones64, 1.0), [], ['ones64'])
    P.op('dve', lambda e: e.memset(epst, EPS), [], ['epst'])
    P.op('dve', lambda e: e.memset(S, 0.0), [], ['S'])
    P.op('dve', lambda e: e.memset(clsum, 0.0), [], ['clsum'])
    for kg in range(2):
        P.op('dve', lambda e, kg=kg: e.tensor_copy(
            out=maskM[0:1, kg, :].rearrange("p (a b) -> p a b", a=4),
            in_=sinks_s[0:1, kg * 4:kg * 4 + 4].unsqueeze(2).broadcast_to([1, 4, 128])),
            ['sinks', 'maskM'], ['maskM'])

    def rms_stats(src, src_key, slot, junk, junk_key):
        ss = stat[:, 4 * slot:4 * slot + 1]
        ln = stat[:, 4 * slot + 1:4 * slot + 2]
        rstd = stat[:, 4 * slot + 2:4 * slot + 3]
        k = f'stat{slot}'
        act(junk, src, AF.Square, [src_key], [junk_key, k], accum=ss)
        act(ln, ss, AF.Ln, [k, 'epst'], [k], bias=epst[:, 0:1], scale=1.0 / D)
        act(rstd, ln, AF.Exp, [k], [k], scale=-0.5)
        return rstd, k

    def norm_transpose(src, src_key, slot, nw, dst, dst_key, utm_bufs, utm_pref):
        u = utm_bufs[slot]
        uk = f'{utm_pref}{slot}'
        rstd, k = rms_stats(src, src_key, slot, u, uk)
        ts('dve', u, src, rstd, None, ALU.mult, None, [src_key, k], [uk])
        transposes([(pT[:, kk * 128:(kk + 1) * 128], u[:, kk * 128:(kk + 1) * 128]) for kk in range(8)],
                   [uk], ['pT'])
        tt('dve', dst, pT[:, :].rearrange("p (a b) -> p a b", a=8),
           nw.unsqueeze(2).broadcast_to([128, 8, 128]), ALU.mult, ['pT', 'cstf'], [dst_key])

    def proj_fm(col0, ncols_m, usl, T, wkey):
        fi = nextF()
        out = Fslot(fi, ncols_m, T)
        mmchain(out, [(win[:, k, col0:col0 + ncols_m], uT[usl][:, k, 0:T]) for k in range(8)],
                [f'uT{usl}.{i}' for i in range((T + 127) // 128)] + [wkey], [f'F{fi}'])
        return out, f'F{fi}'

    def gate_pipeline(usl, T, ntile, need_e1, gs):
        ukeys = [f'uT{usl}.{i}' for i in range(ntile)]
        fi = nextF()
        lo = Fslot(fi, 16, T)
        mmchain(lo, [(win[:, k, C_LR:C_LR + 16], uT[usl][:, k, 0:T]) for k in range(8)],
                ukeys + ['win_tm'], [f'F{fi}'])
        act(lrT[:, 0:T], lo, AF.Copy, [f'F{fi}'], ['lrT'])
        for pc in range(2):
            fz = nextF()
            zo = Fslot(fz, 128, T)
            mmchain(zo, [(wgu[0:16, pc * 128:(pc + 1) * 128], lrT[0:16, 0:T])], ['lrT', 'wgu'], [f'F{fz}'])
            act(esp[:, pc, 0:T], zo, AF.Exp, [f'F{fz}', 'cstf'], [f'esp{pc}'], bias=bgn[:, pc:pc + 1], scale=-1.0)
            act(esp[:, pc, 0:T], esp[:, pc, 0:T], AF.Ln, [f'esp{pc}'], [f'esp{pc}'], bias=1.0)
            P.op('dve', lambda e, pc=pc: e.tensor_tensor_scan(
                out=csp[:, pc, 0:T], data0=resetm[:, 0:T], data1=esp[:, pc, 0:T], initial=0.0,
                op0=ALU.mult, op1=ALU.add), [f'esp{pc}', 'cstf'], [f'csp{pc}'], dur=0.07 + 2 * T / 960.0)
            act(E2[:, pc, 0:T], csp[:, pc, 0:T], AF.Exp, [f'csp{pc}'], [f'E2{pc}'], scale=1.0 / 16)
            if need_e1:
                act(E1[:, pc, 0:T], csp[:, pc, 0:T], AF.Exp, [f'csp{pc}'], [f'E1{pc}'], scale=-1.0 / 16)
        lastc = csp[:, :, 0:T].rearrange("p c (t x) -> p c t x", x=128)[:, :, :, 127]
        act(e1l[gs][:, :, 0:ntile], lastc, AF.Exp, ['csp0', 'csp1'], [f'e1l{gs}'], scale=-1.0 / 16)

    def khat_and_state(ntile, ksrc, kkeys, T, gs):
        for ti in range(ntile):
            cs = slice(ti * 128, (ti + 1) * 128)
            for pc in range(2):
                stt(khT[:, pc, cs], ksrc(pc)[:, cs], e1l[gs][:, pc, ti:ti + 1], E2[:, pc, cs], ALU.mult, ALU.mult,
                    [kkeys(pc), f'e1l{gs}', f'E2{pc}'], [f'khT{pc}.{ti}'])
        for ti in range(ntile):
            cs = slice(ti * 128, (ti + 1) * 128)
            fi = nextF()
            fb_ = FBANK[fi].bitcast(BF16)
            transposes([(fb_[:, pc * 128:(pc + 1) * 128], khT[:, pc, cs]) for pc in range(2)],
                       [f'khT0.{ti}', f'khT1.{ti}'], [f'F{fi}'])
            act(khtm[gs][:, ti, :], fb_[:, 0:256], AF.Copy, [f'F{fi}'], [f'khtm{gs}.{ti}'])

    def state_update(ti, gs, bank=None, bkey=None):
        if bank is None:
            sc = nextSC()
            bank, bkey = pSC[sc], f'SC{sc}'
        mmlist([(bank[:, p * 256:(p + 1) * 256], khtm[gs][:, ti, p * 128:(p + 1) * 128],
                 Vg[gs][:, ti, p * 256:(p + 1) * 256], p == 0, p == 1) for p in range(2)],
               [f'khtm{gs}.{ti}', f'Vg{gs}.{ti}'], [bkey])
        for p in range(2):
            stt(S[:, p, :], S[:, p, :], e1l[gs][:, p, ti:ti + 1], bank[:, p * 256:(p + 1) * 256],
                ALU.mult, ALU.add, ['S', f'e1l{gs}', bkey], ['S'])

    def v_proj(usl, ti, gs):
        fi = nextF()
        bank = FBANK[fi]
        mmchain(bank[:, :], [(uT[usl][:, k, ti * 128:(ti + 1) * 128], win[:, k, C_GV:C_GV + 512]) for k in range(8)],
                [f'uT{usl}.{ti}', 'win_tm'], [f'F{fi}'])
        act(Vg[gs][:, ti, :], bank[:, :], AF.Copy, [f'F{fi}'], [f'Vg{gs}.{ti}'])

    class _Stop(Exception):
        pass

    def rope(ps, pskey, T, tcos, tsin, tkey, scale, out, outkey):
        act(xsb[:, 0:T], ps, AF.Copy, [pskey], ['xsb'])
        if DEBUG_STOP == 'R1':
            raise _Stop(finish_debug_bf(xsb[:, 0:128], 'xsb', 128))
        fr = nextF()
        pr = Fslot(fr, 128, T)
        mmchain(pr, [(Pm, xsb[:, 0:T])], ['xsb', 'cstb'], [f'F{fr}'])
        if DEBUG_STOP == 'R2':
            raise _Stop(finish_debug_bf(pr[:, 0:128], f'F{fr}', 128))
        stt(t1[:, 0:T], ps, scale, tcos, ALU.mult, ALU.mult, [pskey, tkey, 'cstf'] + (['xsb'] if VARX else []), ['t1'])
        if DEBUG_STOP == 'R3':
            raise _Stop(finish_debug(t1[:, 0:128], 't1', 128))
        stt(t2[:, 0:T], pr, scale, tsin, ALU.mult, ALU.mult, [f'F{fr}', tkey, 'cstf'], ['t2'])
        if DEBUG_STOP == 'R4':
            raise _Stop(finish_debug(t2[:, 0:128], 't2', 128))
        tt(PENG, out, t1[:, 0:T], t2[:, 0:T], ALU.add, ['t1', 't2'], [outkey])

    def merge(*lists):
        lists = [l for l in lists if l]
        tot = [sum(o.get('cost', 0) for o in l) or 1 for l in lists]
        idx = [0] * len(lists)
        prog = [0.0] * len(lists)
        out = []
        while True:
            best = None
            for k, l in enumerate(lists):
                if idx[k] < len(l) and (best is None or prog[k] / tot[k] < prog[best] / tot[best]):
                    best = k
            if best is None:
                break
            o = lists[best][idx[best]]
            idx[best] += 1
            prog[best] += o.get('cost', 0)
            out.append(o)
        return out

    esp_s = (esp, E1)
    kf_s = (kf, qf)
    espk = ('esp', 'E1')
    kfk = ('kf', 'qf')
    pObf = pO.bitcast(BF16)

    def Nstage(srcs, usl):
        for ti, (src, skey) in enumerate(srcs):
            norm_transpose(src, skey, ti % 2, nmw, uT[usl][:, :, ti * 128:(ti + 1) * 128], f'uT{usl}.{ti}', utm, 'utm')

    def PjA(T, ntile, usl, gs):
        ukeys = [f'uT{usl}.{i}' for i in range(ntile)]
        fi = nextF()
        lo = Fslot(fi, 16, T)
        mmchain(lo, [(win[:, k, C_LR:C_LR + 16], uT[usl][:, k, 0:T]) for k in range(8)], ukeys + ['win_tm'], [f'F{fi}'])
        act(lrT[:, 0:T], lo, AF.Copy, [f'F{fi}'], ['lrT'])
        for pc in range(2):
            fz = nextF()
            zo = Fslot(fz, 128, T)
            mmchain(zo, [(wgu[0:16, pc * 128:(pc + 1) * 128], lrT[0:16, 0:T])], ['lrT', 'wgu'], [f'F{fz}'])
            act(esp_s[gs][:, pc, 0:T], zo, AF.Exp, [f'F{fz}', 'cstf'], [f'{espk[gs]}{pc}'], bias=bgn[:, pc:pc + 1],
                scale=-1.0)
        for pc in range(2):
            kp, kkey = proj_fm(C_GK + pc * 128, 128, usl, T, 'win_gk')
            act(kf_s[gs][:, pc, 0:T], kp, AF.Copy, [kkey], [f'{kfk[gs]}{pc}'])
        for ti in range(ntile):
            v_proj(usl, ti, gs)

    def A2(T, ntile, gs, own):
        for pc in range(2):
            ek = f'{espk[gs]}{pc}'
            act(esp_s[gs][:, pc, 0:T], esp_s[gs][:, pc, 0:T], AF.Ln, [ek], [ek], bias=1.0)
            P.op('dve', lambda e, pc=pc: e.tensor_tensor_scan(
                out=csp[:, pc, 0:T], data0=resetm[:, 0:T], data1=esp_s[gs][:, pc, 0:T], initial=0.0,
                op0=ALU.mult, op1=ALU.add), [ek, 'cstf'], [f'csp{pc}'], dur=0.07 + 2 * T / 960.0)
            act(E2[:, pc, 0:T], csp[:, pc, 0:T], AF.Exp, [f'csp{pc}'], [f'E2{pc}'], scale=1.0 / 16)
        lastc = csp[:, :, 0:T].rearrange("p c (t x) -> p c t x", x=128)[:, :, :, 127]
        act(e1l[gs][:, :, 0:ntile], lastc, AF.Exp, ['csp0', 'csp1'], [f'e1l{gs}'], scale=-1.0 / 16)
        for ti in range(ntile):
            cs = slice(ti * 128, (ti + 1) * 128)
            for pc in range(2):
                stt(khT[:, pc, cs], kf_s[gs][:, pc, cs], e1l[gs][:, pc, ti:ti + 1], E2[:, pc, cs], ALU.mult, ALU.mult,
                    [f'{kfk[gs]}{pc}', f'e1l{gs}', f'E2{pc}'], [f'khT{pc}.{ti}'])
        for ti in range(ntile):
            cs = slice(ti * 128, (ti + 1) * 128)
            transposes([(pObf[:, pc * 128:(pc + 1) * 128], khT[:, pc, cs]) for pc in range(2)],
                       [f'khT0.{ti}', f'khT1.{ti}'], ['pO0', 'pO1'])
            act(khtm[gs][:, ti, :], pObf[:, 0:256], AF.Copy, ['pO0', 'pO1'], [f'khtm{gs}.{ti}'])
        for ti in range(ntile):
            state_update(ti, gs)
        if own:
            for ti in range(ntile):
                tt('dve', clsum, clsum, lastc[:, :, ti], ALU.add, ['clsum', 'csp0', 'csp1'], ['clsum'])
        else:
            P.op('dve', lambda e: e.tensor_copy(out=Sm, in_=S), ['S'], ['Sm'])
            ts('dve', S[:, :, :], S[:, :, :], flags[:, 3:4], None, ALU.mult, None, ['S', 'cstf'], ['S'])

    GSEQ = [([(xa, 'xa')], 128, 1, False)] + [
        ([(h[:, g * GT + ti, :], f'h{g * GT + ti}') for ti in range(GT)], GW, GT, True) for g in range(NG)]

    def capN(i):
        P.capture()
        Nstage(GSEQ[i][0], i % 2)
        return P.end_capture()

    def capPj(i):
        P.capture()
        PjA(GSEQ[i][1], GSEQ[i][2], i % 2, i % 2)
        return P.end_capture()

    def capA2(i):
        P.capture()
        A2(GSEQ[i][1], GSEQ[i][2], i % 2, GSEQ[i][3])
        return P.end_capture()

    P.play(capN(0))
    P.play_sched([capPj(0), capN(1)])
    for i in range(len(GSEQ)):
        lists = [capA2(i)]
        if i + 1 < len(GSEQ):
            lists.append(capPj(i + 1))
        if i + 2 < len(GSEQ):
            lists.append(capN(i + 2))
        if PIPELINE:
            P.play_sched(lists)
        else:
            for l in lists:
                P.play(l)
    P.op('dve', lambda e: e.tensor_copy(out=pay[:, 0:512], in_=S[:, :, :].rearrange("p a b -> p (a b)")),
         ['S'], ['pay'])
    act(pay[:, 512:514], clsum, AF.Exp, ['clsum', 'pay'], ['pay'], scale=-1.0 / 16)
    dma('pool', cin_d, pay, 'cc0', ['pay'], ['cin'])
    P.dma('pool', lambda e: e.collective_compute("AllGather", ALU.bypass,
                                                  replica_groups=[[0, 1, 2, 3], [4, 5, 6, 7]],
                                                  ins=[cin_d], outs=[cout_d]),
          'cc1', ['cin'], ['cout'], inc=1)
    dma('pool', gath, cout_d[0:3 * 128, :].rearrange("(r p) c -> p r c", p=128), 'cc2', ['cout'], ['gath'])
    def state_chain():
      P.op('dve', lambda e: e.memset(S, 0.0), ['S'], ['S'])
      for j in range(3):
        fj = flags[:, j:j + 1]
        ts('dve', a1, gath[:, j, 512:514], -1.0, fj, ALU.add, ALU.mult, ['gath', 'cstf'], ['a1'])
        ts('dve', a1, a1, 1.0, None, ALU.add, None, ['a1'], ['a1'])
        ts('dve', tg[:, :, :].rearrange("p a b -> p (a b)"), gath[:, j, 0:512], fj, None, ALU.mult, None,
           ['gath', 'cstf'], ['tg'])
        for p in range(2):
            stt(S[:, p, :], S[:, p, :], a1[:, p:p + 1], tg[:, p, :], ALU.mult, ALU.add, ['S', 'a1', 'tg'], ['S'])
      for p in range(2):
        stt(S[:, p, :], Sm[:, p, :], flags[:, 3:4], S[:, p, :], ALU.mult, ALU.add, ['S', 'Sm', 'cstf'], ['S'])
      act(Sb[:, :, :], S[:, :, :], AF.Copy, ['S'], ['Sb'])

    def finish_debug(src_ap, key, width):
        dma('sp', out_d[0:128, 0:width], src_ap, 'dbg', [key], ['dbgout'])
        P._add(dict(eng='sp', fn=None, reads=['dbgout'], writes=[], kind='c', extra=[]))
        P.emit()
        return nc, st

    def finish_debug_bf(src_ap, key, width):
        P.op('dve', lambda e: e.tensor_copy(out=on[:, 0:width], in_=src_ap), [key], ['on'])
        return finish_debug(on[:, 0:width], 'on', width)

    if DEBUG_STOP == 'A0':
        return finish_debug(pay[:, 0:512], 'pay', 512)
    if DEBUG_STOP == 'A':
        state_chain()
        return finish_debug(S[:, :, :].rearrange("p a b -> p (a b)"), 'S', 512)

    for slot in range(2 if not (DEBUG_STOP or '').startswith('R') else 0):
        usl = slot
        if slot == 1:
            dma('sp', xa, xh_d, 'xa0', [], ['xa'])
        norm_transpose(xa, 'xa', slot, nmw, uT[usl][:, :, 0:128], f'uT{usl}.0', utm, 'utm')
        if DEBUG_STOP == f'P{slot}a':
            return finish_debug_bf(uT[usl][:, 0, 0:128], f'uT{usl}.0', 128)
        dma('sp', tabc[slot][:, 0:128], cos_d[:, slot * 128:(slot + 1) * 128], f'tc{slot}', [], [f'tab{slot}'])
        dma('sp', tabs[slot][:, 0:128], sin_d[:, slot * 128:(slot + 1) * 128], f'ts{slot}', [], [f'tab{slot}'])
        kp, kkey = proj_fm(C_SK, 128, usl, 128, 'win_s')
        if DEBUG_STOP == f'P{slot}b':
            return finish_debug_bf(kp, kkey, 128)
        if DEBUG_STOP == f'P{slot}t':
            return finish_debug_bf(tabc[slot][:, 0:128], f'tab{slot}', 128)
        rope(kp, kkey, 128, tabc[slot][:, 0:128], tabs[slot][:, 0:128], f'tab{slot}', c_one, kS[:, slot, :], f'kS{slot}')
        if DEBUG_STOP == f'P{slot}c':
            return finish_debug_bf(kS[:, slot, :], f'kS{slot}', 128)
        fi = nextF()
        vo = Fslot(fi, 128, 128)
        mmchain(vo, [(uT[usl][:, k, 0:128], win[:, k, C_SV:C_SV + 128]) for k in range(8)],
                [f'uT{usl}.0', 'win_tm'], [f'F{fi}'])
        act(VS[:, slot, :], vo, AF.Copy, [f'F{fi}'], [f'VS{slot}'])
        if DEBUG_STOP == f'P{slot}d':
            return finish_debug_bf(VS[:, slot, :], f'VS{slot}', 128)

    if (DEBUG_STOP or '').startswith('R'):
        try:
            slot = usl = 0
            norm_transpose(xa, 'xa', slot, nmw, uT[usl][:, :, 0:128], f'uT{usl}.0', utm, 'utm')
            dma('sp', tabc[slot][:, 0:128], cos_d[:, 0:128], 'tc0', [], ['tab0'])
            dma('sp', tabs[slot][:, 0:128], sin_d[:, 0:128], 'ts0', [], ['tab0'])
            kp, kkey = proj_fm(C_SK, 128, usl, 128, 'win_s')
            rope(kp, kkey, 128, tabc[0][:, 0:128], tabs[0][:, 0:128], 'tab0', c_one, kS[:, 0, :], 'kS0')
        except _Stop as ex:
            return ex.args[0]
    if DEBUG_STOP == 'B0':
        return finish_debug_bf(kS[:, 0:2, :].rearrange("p a b -> p (a b)"), 'kS1', 256)
    if DEBUG_STOP == 'B0v':
        return finish_debug_bf(VS[:, 0:2, :].rearrange("p a b -> p (a b)"), 'VS1', 256)
    def kskey(slot):
        return f'kS{slot}' if slot < 2 else f'kSg{(slot - 2) // GT}'

    def nfront(g):
        usl = g % 2
        for ti in range(GT):
            t = g * GT + ti
            norm_transpose(h[:, t, :], f'h{t}', ti, nmw, uT[usl][:, :, ti * 128:(ti + 1) * 128], f'uT{usl}.{ti}',
                           utm, 'utm')

    def front(g):
        usl = tsl = gs = g % 2
        T = GW
        dma('sp', tabc[tsl], cos_d[:, 256 + g * GW:256 + (g + 1) * GW], f'tc{tsl}', [], [f'tab{tsl}'])
        dma('sp', tabs[tsl], sin_d[:, 256 + g * GW:256 + (g + 1) * GW], f'ts{tsl}', [], [f'tab{tsl}'])
        gate_pipeline(usl, T, GT, True, gs)
        for pc in range(2):
            kp, kkey = proj_fm(C_GK + pc * 128, 128, usl, T, 'win_gk')
            act(kf[:, pc, :], kp, AF.Copy, [kkey], [f'kf{pc}'])
        for pc in range(2):
            qp, qkey = proj_fm(C_GQ + pc * 128, 128, usl, T, 'win_gq')
            act(qf[:, pc, :], qp, AF.Copy, [qkey], [f'qf{pc}'], scale=0.125)
        for pc in range(2):
            tt(PENG, qtl[gs][:, pc, :], qf[:, pc, :], E1[:, pc, :], ALU.mult, [f'qf{pc}', f'E1{pc}'], [f'qtl{gs}.{pc}'])
            tt(PENG, ktl[gs][:, pc, :], kf[:, pc, :], E2[:, pc, :], ALU.mult, [f'kf{pc}', f'E2{pc}'], [f'ktl{gs}.{pc}'])
        khat_and_state(GT, lambda pc: kf[:, pc, :], lambda pc: f'kf{pc}', T, gs)
        for c in range(4):
            rp, rkey = proj_fm(C_GR + c * 128, 128, usl, T, 'win_gr')
            act(sg[gs][:, c, :], rp, AF.Silu, [rkey], [f'sg{gs}.{c}'])
        for c in range(4):
            qp, qkey = proj_fm(C_SQ + c * 128, 128, usl, T, 'win_s')
            rope(qp, qkey, T, tabc[tsl], tabs[tsl], f'tab{tsl}', c_eighth, qr[gs][:, c, :], f'qr{gs}.{c}')
        kp, kkey = proj_fm(C_SK, 128, usl, T, 'win_s')
        rope(kp, kkey, T, tabc[tsl], tabs[tsl], f'tab{tsl}', c_one,
             kS[:, 2 + g * GT:2 + (g + 1) * GT, :].rearrange("p a b -> p (a b)"), f'kSg{g}')
        for ti in range(GT):
            t = g * GT + ti
            v_proj(usl, ti, gs)
            fi = nextF()
            vo = Fslot(fi, 128, 128)
            mmchain(vo, [(uT[usl][:, k, ti * 128:(ti + 1) * 128], win[:, k, C_SV:C_SV + 128]) for k in range(8)],
                    [f'uT{usl}.{ti}', 'win_tm'], [f'F{fi}'])
            act(VS[:, 2 + t, :], vo, AF.Copy, [f'F{fi}'], [f'VS{2 + t}'])

    def wout_tile(t):
        ms = t % 2
        mk = f'mix{ms}'
        for half in range(2):
            if SPLIT_BACK:
                wb, wk = ((pSC[1], 'SC1'), (pW, 'F2'))[half]
            else:
                sc = nextSC()
                wb, wk = pSC[sc], f'SC{sc}'
            mmchain(wb[:, :], [(mixT[ms][:, c, :], wout[:, c, half * 512:(half + 1) * 512]) for c in range(8)],
                    [mk + 's', mk + 'g', 'wout'], [wk])
            tt('dve', h[:, t, half * 512:(half + 1) * 512], wb[:, :], h[:, t, half * 512:(half + 1) * 512],
               ALU.add, [wk, f'h{t}'], [f'h{t}'])

    def swa_tile(t):
        g, ti = t // GT, t % GT
        gs = g % 2
        ms = t % 2
        cs = slice(ti * 128, (ti + 1) * 128)
        mk = f'mix{ms}'
        qkeys = [f'qr{gs}.{j}' for j in range(4)]
        blocks = [(1 + t, maskH if t == 0 else maskP), (2 + t, maskC), (0, None)]
        slots = [b[0] for b in blocks]
        for kg in range(2):
            pb = 64 * kg
            ptb = PT[kg]
            for bi, (slot, msk) in enumerate(blocks):
                sc = 0 if SPLIT_BACK else nextSC()
                if bi < 2:
                    mrhs = msk.unsqueeze(1).broadcast_to([128, 4, 128])
                    mout = pSC[sc][:, :].rearrange("p (a b) -> p a b", a=4)
                else:
                    mrhs = maskM[:, kg, :]
                    mout = pSC[sc][:, :]
                items = [(mout, ident, mrhs, True, False),
                         (pSC[sc][:, :].rearrange("p (a b) -> p a b", a=4), kS[pb:pb + 64, slot, :],
                          qr[gs][pb:pb + 64, :, cs], False, True)]
                mmlist(items, ['cstb', 'maskM', kskey(slot)] + qkeys, [f'SC{sc}'])
                act(ptb[:, bi, :], pSC[sc][:, :], AF.Exp, [f'SC{sc}'], [f'PT{kg}.{bi}'])
        for kg in range(2):
            pb = 64 * kg
            ptb = PT[kg]
            mmlist([(pO[pb:pb + 64, :], VS[:, slots[bi], pb:pb + 64], ptb[:, bi, :], bi == 0, bi == 2)
                    for bi in range(3)],
                   [f'VS{s_}' for s_ in slots] + [f'PT{kg}.{bi}' for bi in range(3)], [f'pO{kg}'])
            mmlist([(pD[pb:pb + 64, :], ones64, ptb[:, bi, :], bi == 0, bi == 2) for bi in range(3)],
                   ['ones64'] + [f'PT{kg}.{bi}' for bi in range(3)], [f'pD{kg}'])
        act(rden, pD[:, :], AF.Ln, ['pD0', 'pD1'], ['rden'])
        act(rden, rden, AF.Exp, ['rden'], ['rden'], scale=-1.0)
        tt('dve', mixT[ms][:, 4:8, :], pO[:, :].rearrange("p (a b) -> p a b", a=4),
           rden[:, :].rearrange("p (a b) -> p a b", a=4), ALU.mult, ['pO0', 'pO1', 'rden'], [mk + 's'])

    def gla_tile(t):
        g, ti = t // GT, t % GT
        gs = g % 2
        ms = t % 2
        cs = slice(ti * 128, (ti + 1) * 128)
        mk = f'mix{ms}'
        qk = [f'qtl{gs}.0', f'qtl{gs}.1']
        kk = [f'ktl{gs}.0', f'ktl{gs}.1']
        if SPLIT_BACK:
            AB = (pSC[1], pW)
            AK = (['SC1'], ['F2'])
            GOB, GOK = AB, AK
        else:
            AB = (pSC[0], pSC[1])
            AK = (['SC0'], ['SC1'])
            GOB = (pO, pD)
            GOK = (['pO0', 'pO1'], ['pD0', 'pD1'])
        items = []
        for hh in range(4):
            p_, r0 = hh // 2, 64 * (hh % 2)
            col = (hh // 2) * 128
            items.append((AB[hh % 2][:, col:col + 128], ktl[gs][r0:r0 + 64, p_, cs], qtl[gs][r0:r0 + 64, p_, cs],
                          hh < 2, hh >= 2))
        mmlist(items, kk + qk, AK[0] + AK[1])
        ATv = AT[:, :].rearrange("p (a b c) -> p a b c", a=2, b=2)
        for par in range(2):
            tt('dve', ATv[:, :, par, :], AB[par][:, 0:256].rearrange("p (a c) -> p a c", a=2),
               causal.unsqueeze(1).broadcast_to([128, 2, 128]), ALU.mult, AK[par] + ['cstb'], ['AT'])
        items = []
        for hh in (0, 2, 1, 3):
            p_, r0 = hh // 2, 64 * (hh % 2)
            col = (hh // 2) * 128
            o_ = GOB[hh % 2][:, col:col + 128]
            items.append((o_, Vg[gs][:, ti, hh * 128:(hh + 1) * 128], AT[:, hh * 128:(hh + 1) * 128],
                          hh // 2 == 0, False))
            items.append((o_, Sb[r0:r0 + 64, p_, (hh % 2) * 128:(hh % 2) * 128 + 128], qtl[gs][r0:r0 + 64, p_, cs],
                          False, hh // 2 == 1))
        mmlist(items, [f'Vg{gs}.{ti}', 'AT', 'Sb'] + qk, GOK[0] + GOK[1])
        sqv = sq[:, :].rearrange("p (a b c) -> p a b c", a=2, b=2)
        onv = on[:, :].rearrange("p (a b c) -> p a b c", a=2, b=2)
        rsv = rs[:, :].rearrange("p (a b c) -> p a b c", a=2, b=2)
        if SPLIT_BACK:
            for par in range(2):
                src = GOB[par][:, 0:256].rearrange("p (a c) -> p a c", a=2)
                act(sqv[:, :, par, :], src, AF.Square, GOK[par], ['sq'])
                ts('dve', onv[:, :, par, :], src, gnw, None, ALU.mult, None, GOK[par] + ['cstf'], ['on'])
            state_update(ti, gs, GOB[0], GOK[0][0])
            act(Sb[:, :, :], S[:, :, :], AF.Copy, ['S'], ['Sb'])
            mmchain(GOB[1][:, :], [(onesdiv, sq)], ['sq', 'cstb'], GOK[1])
            act(rs, GOB[1][:, :], AF.Ln, GOK[1] + ['epst'], ['rs'], bias=epst[:, 0:1])
            act(rs, rs, AF.Exp, ['rs'], ['rs'], scale=-0.5)
            tt('dve', on, on, rs, ALU.mult, ['on', 'rs'], ['on'])
        else:
            state_update(ti, gs)
            act(Sb[:, :, :], S[:, :, :], AF.Copy, ['S'], ['Sb'])
            for par in range(2):
                act(sqv[:, :, par, :], GOB[par][:, 0:256].rearrange("p (a c) -> p a c", a=2), AF.Square,
                    GOK[par], ['sq'])
            sc = nextSC()
            mmchain(pSC[sc][:, :], [(onesdiv, sq)], ['sq', 'cstb'], [f'SC{sc}'])
            act(rs, pSC[sc][:, :], AF.Ln, [f'SC{sc}', 'epst'], ['rs'], bias=epst[:, 0:1])
            act(rs, rs, AF.Exp, ['rs'], ['rs'], scale=-0.5)
            for par in range(2):
                stt(onv[:, :, par, :], GOB[par][:, 0:256].rearrange("p (a c) -> p a c", a=2), gnw, rsv[:, :, par, :],
                    ALU.mult, ALU.mult, GOK[par] + ['cstf', 'rs'], ['on'])
        tt(PENG, mixT[ms][:, 0:4, :], on[:, :].rearrange("p (a b) -> p a b", a=4), sg[gs][:, :, cs], ALU.mult,
           ['on'] + [f'sg{gs}.{c}' for c in range(4)], [mk + 'g'])

    def back(g):
        for ti in range(GT):
            t = g * GT + ti
            swa_tile(t)
            if t > 0:
                wout_tile(t - 1)
            gla_tile(t)

    def back_swa(g):
        for ti in range(GT):
            swa_tile(g * GT + ti)

    def back_gla(g):
        for ti in range(GT):
            t = g * GT + ti
            if t > 0:
                wout_tile(t - 1)
            gla_tile(t)

    def cap(fn, *a):
        P.capture()
        fn(*a)
        return P.end_capture()

    fmode[0] = 'B'
    P.play(cap(nfront, 0))
    P.play_sched([cap(front, 0), cap(nfront, 1)])
    state_chain()
    for g in range(NG):
        lists = [cap(back_swa, g), cap(back_gla, g)] if SPLIT_BACK else [cap(back, g)]
        if g + 1 < NG:
            lists.append(cap(front, g + 1))
        if g + 2 < NG:
            lists.append(cap(nfront, g + 2))
        if PIPELINE:
            P.play_sched(lists)
        else:
            for l in lists:
                P.play(l)
    wout_tile(NT - 1)

    if DEBUG_STOP == 'B':
        return finish_debug(h[:, NT - 1, :], f'h{NT - 1}', 1024)
    fmode[0] = 'C'
    w1_v = w1_d.rearrange("(k p) f -> p k f", p=128)

    def load_w1(fb, extra=()):
        s_ = fb % 2
        P.dma('pool', lambda e: e.dma_start(out=W1b[s_], in_=w1_v[:, :, fb * FB:(fb + 1) * FB]), f'f1{s_}', [], [f'W1{s_}'],
              extra=extra, dur=40.0)

    def load_w2(fb, extra=()):
        s_ = fb % 2
        P.dma('pool', lambda e: e.dma_start(out=W2b[s_], in_=w2_d[fb * FB:(fb + 1) * FB, :].rearrange("(c p) d -> p c d", p=128)),
              f'f2{s_}', [], [f'W2{s_}'], extra=extra, dur=40.0)

    load_w1(0, extra=[P.last_win])
    load_w2(0, extra=[P.last_win])

    lastB = [P.last_on[e] for e in ('pe', 'act', 'dve', 'pool', 'sp') if e in P.last_on]
    for e in ('pe', 'act', 'dve', 'pool', 'sp'):
        P._add(dict(eng=e, fn=None, reads=[], writes=[], kind='c', extra=list(lastB)))

    load_w1(1)
    load_w2(1)
    dma('sp', finw, finw_d, 'fw', [], ['finw'])

    def make_fT(t):
        norm_transpose(h[:, t, :], f'h{t}', t % 2, nfw, fT[:, :, t * 128:(t + 1) * 128], f'fT{t}', utmC, 'utmC')

    cctr = [0]

    def ffn1(u):
        fb, g4 = divmod(u, 4)
        s_, hs = fb % 2, u % 2
        tcs = slice(g4 * 512, (g4 + 1) * 512)
        for fc in range(8):
            bi = cctr[0] % 2
            cctr[0] += 1
            bank = pF[bi]
            mmchain(bank[:, :], [(W1b[s_][:, k, fc * 128:(fc + 1) * 128], fT[:, k, tcs]) for k in range(8)],
                    [f'W1{s_}'] + [f'fT{g4 * 4 + i}' for i in range(4)], [f'F{bi}'])
            act(rtmp[bi], bank[:, :], AF.Relu, [f'F{bi}'], [f'rt{bi}'])
            tt(PENG, h1T[hs][:, fc, :], rtmp[bi], rtmp[bi], ALU.mult, [f'rt{bi}'], [f'h1T{hs}.{fc}'])
        if g4 == 3 and fb + 2 < NFB:
            load_w1(fb + 2)

    def final_tile(t):
        s_ = t % 2
        rstd, k = rms_stats(h[:, t, :], f'h{t}', 2 + s_, junkD[s_], f'junkD{s_}')
        stt(yout[s_], h[:, t, :], rstd, finw, ALU.mult, ALU.mult, [f'h{t}', k, 'finw'], [f'y{s_}'])
        dma('sp', out_d[t * 128:(t + 1) * 128, :], yout[s_], f'o{s_}', [f'y{s_}'], [f'out{t}'])

    def ffn2(u):
        fb, g4 = divmod(u, 4)
        s_, hs = fb % 2, u % 2
        for ti in range(4):
            t = g4 * 4 + ti
            for half in range(2):
                ob = (pSC[0], pSC[1], pO, pD)[(ti % 2) * 2 + half]
                okey = ('SC0', 'SC1', 'pO0', 'pD0')[(ti % 2) * 2 + half]
                mmchain(ob[:, :], [(h1T[hs][:, fc, ti * 128:(ti + 1) * 128], W2b[s_][:, fc, half * 512:(half + 1) * 512])
                                   for fc in range(8)],
                        [f'W2{s_}'] + [f'h1T{hs}.{fc}' for fc in range(8)], [okey])
                tt('dve', h[:, t, half * 512:(half + 1) * 512], ob[:, :], h[:, t, half * 512:(half + 1) * 512],
                   ALU.add, [okey, f'h{t}'], [f'h{t}'])
            if fb == NFB - 1:
                final_tile(t)
        if g4 == 3 and fb + 2 < NFB:
            load_w2(fb + 2)

    for t in range(4):
        make_fT(t)
    NU = NFB * 4

    def ffn_stream():
        ffn1(0)
        for u in range(NU):
            if u + 1 < NU:
                ffn1(u + 1)
            ffn2(u)

    def fT_rest():
        for t in range(4, NT):
            make_fT(t)

    P.play_sched([cap(ffn_stream), cap(fT_rest)])
    P._add(dict(eng='sp', fn=None, reads=[f'out{t}' for t in range(NT)], writes=[], kind='c', extra=[]))

    global LAST_PROG
    LAST_PROG = P
    P.emit()
    return nc, st


def _host_constants(seg):
    f32 = np.float32
    jj = np.arange(128)[:, None]
    rr = np.arange(128)[None, :]
    ident = np.eye(128, dtype=f32)
    maskP = np.where(jj > rr, 0.0, NEG).astype(f32)
    maskC = np.where(jj <= rr, 0.0, NEG).astype(f32)
    maskH = maskP if seg > 0 else np.full((128, 128), NEG, f32)
    causal = (jj <= rr).astype(f32)
    onesdiv = np.full((128, 128), 1.0 / 128, f32)
    Pm = np.zeros((128, 128), f32)
    for m in range(128):
        dm = m % 64
        if dm < 8:
            Pm[m + 8, m] = 1.0
        elif dm < 16:
            Pm[m - 8, m] = 1.0
    cstb = np.concatenate([ident, maskP, maskC, maskH, causal, onesdiv, Pm], axis=1)
    maskm = np.zeros((128, 1024), f32)
    maskm[1:112, :] = NEG
    inv_freq = (1.0 / (f32(ROPE_THETA) ** (np.arange(0, 16, 2, dtype=f32) / f32(16)))).astype(f32)
    t0 = seg * TOK
    pos = np.zeros(2304, np.int64)
    pos[112:128] = np.arange(16)
    pos[128:256] = np.maximum(N_META + t0 - 128 + np.arange(128), 0)
    pos[256:] = N_META + t0 + np.arange(TOK)
    ang = (pos.astype(f32)[None, :] * inv_freq[:, None]).astype(f32)
    cosv = np.cos(ang.astype(np.float64)).astype(f32)
    sinv = np.sin(ang.astype(np.float64)).astype(f32)
    cosT = np.ones((128, 2304), f32)
    sinT = np.zeros((128, 2304), f32)
    for p in range(128):
        dm = p % 64
        if dm < 16:
            cosT[p] = cosv[dm % 8]
            sinT[p] = -sinv[dm % 8] if dm < 8 else sinv[dm % 8]
    return cstb, maskm, cosT, sinT


_CACHE = {}


def kernel(x, meta_tokens, norm_mix_w, w_in, w_gate_up, b_gate, gla_norm_w, sinks, w_out, norm_ff_w,
           w_ff1, w_ff2, final_norm_w):
    f32 = np.float32
    x = np.asarray(x, f32)
    meta = np.asarray(meta_tokens, f32)
    w_in0 = np.asarray(w_in, f32)[0]
    w_out0 = np.asarray(w_out, f32)[0]
    o_gq, o_gk, o_gv, o_gr, o_lr, o_sq, o_sk, o_sv = 0, 256, 512, 1024, 1536, 1552, 2064, 2192
    sq_cols = []
    for j in range(4):
        for hd in (j, 4 + j):
            sq_cols.extend(range(o_sq + hd * 64, o_sq + hd * 64 + 64))
    cols = (list(range(o_gq, o_gq + 256)) + list(range(o_gk, o_gk + 256)) + list(range(o_gr, o_gr + 512))
            + sq_cols + list(range(o_sk, o_sk + 128)) + list(range(o_gv, o_gv + 512))
            + list(range(o_sv, o_sv + 128)) + list(range(o_lr, o_lr + 16)))
    assert len(cols) == NCOL
    w_in_p = np.ascontiguousarray(w_in0[:, cols])
    rows = list(range(512))
    for j in range(4):
        for hd in (j, 4 + j):
            rows.extend(range(512 + hd * 64, 512 + hd * 64 + 64))
    w_out_p = np.ascontiguousarray(w_out0[rows, :])

    cstf_base = np.zeros((128, CF_N), f32)
    cstf_base[:, CF_RESET:CF_RESET + 256] = 1.0
    cstf_base[:, CF_RESET + 0] = 0.0
    cstf_base[:, CF_RESET + 128] = 0.0
    cstf_base[:, CF_NMW:CF_NMW + 8] = np.asarray(norm_mix_w, f32)[0].reshape(8, 128).T
    cstf_base[:, CF_NFW:CF_NFW + 8] = np.asarray(norm_ff_w, f32)[0].reshape(8, 128).T
    cstf_base[:, CF_ONE] = 1.0
    cstf_base[:, CF_EIGHTH] = 0.125
    cstf_base[:, CF_GNW] = np.asarray(gla_norm_w, f32)[0]
    cstf_base[:, CF_BGN:CF_BGN + 2] = np.asarray(b_gate, f32)[0].reshape(2, 128).T
    finw = np.ascontiguousarray(np.broadcast_to(np.asarray(final_norm_w, f32)[None, :], (128, D)))
    xm = np.zeros((128, D), f32)
    xm[112:128] = meta

    in_maps = []
    for c in range(NCORES):
        b, s = c // NSEG, c % NSEG
        t0 = s * TOK
        cstb, maskm, cosT, sinT = _host_constants(s)
        cstf = cstf_base.copy()
        for j in range(3):
            cstf[:, CF_FLAGS + j] = 1.0 if j < s else 0.0
        cstf[:, CF_FLAGS + 3] = 1.0 if s == 0 else 0.0
        xh = x[b, t0 - 128:t0] if s > 0 else np.zeros((128, D), f32)
        in_maps.append({
            "xo": np.ascontiguousarray(x[b, t0:t0 + TOK]),
            "xh": np.ascontiguousarray(xh),
            "xm": xm,
            "w_in": w_in_p,
            "w_out": w_out_p,
            "w_ff1": np.ascontiguousarray(np.asarray(w_ff1, f32)[0]),
            "w_ff2": np.ascontiguousarray(np.asarray(w_ff2, f32)[0]),
            "wgu": np.ascontiguousarray(np.asarray(w_gate_up, f32)[0]),
            "cstb": cstb,
            "maskm": maskm,
            "cstf": cstf,
            "finw": finw,
            "sinks": np.ascontiguousarray(np.asarray(sinks, f32).reshape(1, 8)),
            "cosT": cosT,
            "sinT": sinT,
        })
    if 'nc' not in _CACHE:
        _CACHE['nc'] = build_program()
    nc, _st = _CACHE['nc']
    res = run_bass_kernel_spmd(nc, in_maps, core_ids=list(range(NCORES)))
    out = np.zeros((BATCH, SEQ, D), f32)
    for c in range(NCORES):
        b, s = c // NSEG, c % NSEG
        out[b, s * TOK:(s + 1) * TOK] = res.results[c]["out"]
    return out
```

```python
import numpy as np
from contextlib import ExitStack

import concourse.bass as bass
import concourse.mybir as mybir
from concourse.bass_utils import run_bass_kernel_spmd

F32 = mybir.dt.float32
BF16 = mybir.dt.bfloat16
AF = mybir.ActivationFunctionType
ALU = mybir.AluOpType

NCORES = 8
D = 1024
SEQ = 8192
BATCH = 2
NSEG = 4
TOK = SEQ // NSEG
NT = TOK // 128
GT = 2
NG = NT // GT
GW = GT * 128
N_META = 16
DFF = 4096
FB = 1024
NFB = DFF // FB
EPS = 1e-5
NEG = -30000.0
ROPE_THETA = 500000.0
DEBUG_STOP = None
PENG = 'dve'
VARX = False
PIPELINE = True

C_GQ, C_GK, C_GR, C_SQ, C_SK, C_GV, C_SV, C_LR = 0, 256, 512, 1024, 1536, 1664, 2176, 2304
NCOL = 2320
CF_RESET, CF_FLAGS, CF_NMW, CF_NFW, CF_GNW, CF_BGN = 0, 256, 264, 272, 280, 281
CF_ONE, CF_EIGHTH = 283, 284
CF_N = 285


SYNC_LAT = 0.27
SPLIT_BACK = False
LIST_BIAS = 0.0
TABLE_PENALTY = 0.0
PSUM_KEYS = {'pT', 'F0', 'F1', 'F2', 'SC0', 'SC1', 'pO0', 'pO1', 'pD0', 'pD1'}


class Prog:
    def __init__(self, nc, stack):
        self.nc = nc
        self.stack = stack
        self.ops = []
        self.last_w = {}
        self.readers = {}
        self.last_on = {}
        self.dsem = {}
        self.esem = {}
        self._cap = None
        self.fin = []
        self.eng_free = {}
        self.act_table = None

    def _deps_of(self, o):
        d = set()
        for r in o['reads']:
            j = self.last_w.get(r)
            if j is not None:
                d.add(j)
        for w in o['writes']:
            j = self.last_w.get(w)
            if j is not None:
                d.add(j)
            d.update(self.readers.get(w, ()))
        d.update(o['extra'])
        return d

    def est_start(self, o):
        t = self.eng_free.get(o['eng'], 0.0)
        for j in self._deps_of(o):
            lat = 0.03 if (self.ops[j]['eng'] == o['eng'] and self.ops[j]['kind'] == 'c') else SYNC_LAT
            t = max(t, self.fin[j] + lat)
        tb = o.get('table')
        if tb is not None and tb != self.act_table:
            t += 1.3 + TABLE_PENALTY
        return t

    def play_sched(self, lists):
        lists = [l for l in lists if l]
        idx = [0] * len(lists)
        while True:
            best, bt = None, None
            for k, l in enumerate(lists):
                if idx[k] < len(l):
                    t = self.est_start(l[idx[k]]) + k * LIST_BIAS
                    if best is None or t < bt - 1e-9:
                        best, bt = k, t
            if best is None:
                break
            self._add(lists[best][idx[best]])
            idx[best] += 1

    def capture(self):
        self._cap = []

    def end_capture(self):
        c, self._cap = self._cap, None
        return c

    def play(self, ops):
        for o in ops:
            self._add(o)

    def _add(self, o):
        if self._cap is not None:
            self._cap.append(o)
            return None
        i = len(self.ops)
        raw, oth = set(), set()
        for r in o['reads']:
            j = self.last_w.get(r)
            if j is not None:
                raw.add(j)
        for w in o['writes']:
            j = self.last_w.get(w)
            if j is not None:
                oth.add(j)
            for rr in self.readers.get(w, ()):
                oth.add(rr)
        for e in o['extra']:
            raw.add(e)
        oth -= raw
        o['raw'], o['oth'] = raw, oth
        o['prod'] = {r: self.ops[self.last_w[r]].get('label') for r in o['reads'] if r in self.last_w}
        for r in o['reads']:
            self.readers.setdefault(r, []).append(i)
        for r in o['reads']:
            if r in PSUM_KEYS and r not in o['writes']:
                self.last_w[r] = i
                self.readers[r] = []
        for w in o['writes']:
            self.last_w[w] = i
            self.readers[w] = []
        t0 = self.eng_free.get(o['eng'], 0.0)
        for j in raw | oth:
            lat = 0.03 if (self.ops[j]['eng'] == o['eng'] and self.ops[j]['kind'] == 'c') else SYNC_LAT
            t0 = max(t0, self.fin[j] + lat)
        dur = o.get('dur', 0.3)
        tb = o.get('table')
        if tb is not None and tb != self.act_table:
            t0 += 1.3
            self.act_table = tb
        if o['kind'] == 'd':
            self.eng_free[o['eng']] = t0 + 0.15
        else:
            self.eng_free[o['eng']] = t0 + dur
        self.fin.append(t0 + dur)
        self.ops.append(o)
        self.last_on[o['eng']] = i
        if any(r.startswith('win_') for r in o['reads']):
            self.last_win = i
        return i

    def op(self, eng, fn, reads=(), writes=(), extra=(), cost=0, dur=None, table=None):
        if dur is None:
            dur = cost / 1950.0 if eng == 'pe' else 0.3
        return self._add(dict(eng=eng, fn=fn, reads=list(reads), writes=list(writes), kind='c',
                              extra=list(extra), cost=cost, dur=dur, table=table))

    def dma(self, eng, fn, sem, reads=(), writes=(), extra=(), inc=16, dur=4.0):
        if sem not in self.dsem:
            self.dsem[sem] = self.stack.enter_context(self.nc.semaphore("d_" + sem))
        return self._add(dict(eng=eng, fn=fn, reads=list(reads), writes=list(writes), kind='d',
                              sem=sem, inc=inc, extra=list(extra), dur=dur))

    def emit(self):
        nc, ops = self.nc, self.ops
        engs = ['pe', 'act', 'dve', 'pool', 'sp']
        for e in engs:
            self.esem[e] = self.stack.enter_context(nc.semaphore("e_" + e))
        need = [False] * len(ops)
        for i, o in enumerate(ops):
            E = o['eng']
            wl = []
            for j in sorted(o['raw'] | o['oth']):
                pj = ops[j]
                if pj['kind'] == 'd':
                    wl.append(j)
                elif pj['eng'] == E:
                    if o['kind'] == 'd' or E in ('act', 'dve', 'pool'):
                        wl.append(j)
                else:
                    wl.append(j)
            o['wl'] = wl
            for j in wl:
                if ops[j]['kind'] == 'c':
                    need[j] = True
        cnt = {}
        cum = {}
        for i, o in enumerate(ops):
            if o['kind'] == 'c':
                if need[i]:
                    cnt[o['eng']] = cnt.get(o['eng'], 0) + 1
                    o['ms'] = cnt[o['eng']]
            else:
                o['cb'] = cum.get(o['sem'], 0)
                cum[o['sem']] = o['cb'] + o['inc']
                o['cum'] = cum[o['sem']]

        def emit_engine(E, eng):
            seen = {}
            for i, o in enumerate(ops):
                if o['eng'] != E:
                    continue
                tgt = {}
                for j in o['wl']:
                    pj = ops[j]
                    if pj['kind'] == 'd':
                        k, v, s = ('d', pj['sem']), pj['cum'], self.dsem[pj['sem']]
                    else:
                        k, v, s = ('c', pj['eng']), pj['ms'], self.esem[pj['eng']]
                    if v > tgt.get(k, (0, None))[0]:
                        tgt[k] = (v, s)
                if o['kind'] == 'd' and o['cb'] > 0:
                    k = ('d', o['sem'])
                    if o['cb'] > tgt.get(k, (0, None))[0]:
                        tgt[k] = (o['cb'], self.dsem[o['sem']])
                for k, (v, s) in tgt.items():
                    if v > seen.get(k, 0):
                        eng.wait_ge(s, v)
                        seen[k] = v
                if o['fn'] is None:
                    continue
                ins = o['fn'](eng)
                if o['kind'] == 'd':
                    ins.then_inc(self.dsem[o['sem']], o['inc'])
                elif need[i]:
                    ins.then_inc(self.esem[E], 1)

        with nc.Block() as block:
            @block.tensor
            def _(e):
                emit_engine('pe', e)

            @block.scalar
            def _(e):
                emit_engine('act', e)

            @block.vector
            def _(e):
                emit_engine('dve', e)

            @block.gpsimd
            def _(e):
                emit_engine('pool', e)

            @block.sync
            def _(e):
                emit_engine('sp', e)


def build_program():
    nc = bass.Bass("TRN2", target_bir_lowering=False, dynamic_dma_scratch_size=8192)
    st = ExitStack()
    P = Prog(nc, st)

    def dram(name, shape, dt=F32, kind="ExternalInput"):
        return nc.dram_tensor(name, shape, dt, kind=kind).ap()

    def sb(name, shape, dt):
        t = st.enter_context(nc.sbuf_tensor(name, shape, dt))
        return t[tuple(slice(None) for _ in shape)]

    def psum(name, shape, dt):
        t = st.enter_context(nc.psum_tensor(name, shape, dt))
        return t[tuple(slice(None) for _ in shape)]

    xo_d = dram("xo", [TOK, D])
    xh_d = dram("xh", [128, D])
    xm_d = dram("xm", [128, D])
    win_d = dram("w_in", [D, NCOL])
    wout_d = dram("w_out", [D, D])
    w1_d = dram("w_ff1", [D, DFF])
    w2_d = dram("w_ff2", [DFF, D])
    wgu_d = dram("wgu", [16, 256])
    cstb_d = dram("cstb", [128, 7 * 128])
    maskm_d = dram("maskm", [128, 1024])
    cstf_d = dram("cstf", [128, CF_N])
    finw_d = dram("finw", [128, D])
    sinks_d = dram("sinks", [1, 8])
    cos_d = dram("cosT", [128, 2304])
    sin_d = dram("sinT", [128, 2304])
    out_d = dram("out", [TOK, D], kind="ExternalOutput")
    cin_d = dram("cc_in", [128, 514], kind="Internal")
    cout_d = dram("cc_out", [4 * 128, 514], kind="Internal")

    h = sb("h", [128, NT, D], F32)
    cstb = sb("cstb_s", [128, 7, 128], BF16)
    ident, maskP, maskC, maskH, causal, onesdiv, Pm = [cstb[:, i, :] for i in range(7)]
    maskM = sb("maskM", [128, 2, 512], BF16)
    cstf = sb("cstf_s", [128, CF_N], F32)
    resetm = cstf[:, CF_RESET:CF_RESET + 256]
    flags = cstf[:, CF_FLAGS:CF_FLAGS + 8]
    nmw = cstf[:, CF_NMW:CF_NMW + 8]
    nfw = cstf[:, CF_NFW:CF_NFW + 8]
    gnw = cstf[:, CF_GNW:CF_GNW + 1]
    bgn = cstf[:, CF_BGN:CF_BGN + 2]
    c_one = cstf[:, CF_ONE:CF_ONE + 1]
    c_eighth = cstf[:, CF_EIGHTH:CF_EIGHTH + 1]
    wgu32 = sb("wgu_s", [16, 256], F32)
    wgu = sb("wgu_b", [16, 256], BF16)
    sinks_s = sb("sinks_s", [1, 8], F32)
    ones64 = sb("ones64", [128, 64], BF16)
    epst = sb("epst", [128, 1], F32)
    stat = sb("stat", [128, 16], F32)

    ARENA = 143 * 1024
    arena = sb("arena", [128, ARENA // 4], F32)
    apos = [0]

    def carve(shape, dt):
        n = int(np.prod(shape[1:]))
        nbytes = n * (4 if dt == F32 else 2)
        nbytes = (nbytes + 31) // 32 * 32
        off = apos[0]
        apos[0] += nbytes
        assert apos[0] <= ARENA, ("arena overflow", apos[0])
        base = arena[0:shape[0], off // 4:(off + nbytes) // 4]
        if dt != F32:
            base = base.bitcast(dt)
        ap = base[:, 0:n]
        if len(shape) == 3:
            ap = ap.rearrange("p (a b) -> p a b", a=shape[1])
        return ap

    win = carve([128, 8, NCOL], BF16)
    wout = carve([128, 8, D], BF16)
    tabc = [carve([128, GW], F32) for _ in range(2)]
    tabs = [carve([128, GW], F32) for _ in range(2)]
    utm = [carve([128, D], BF16) for _ in range(2)]
    uT = [carve([128, 8, GW], BF16) for _ in range(2)]
    xa = carve([128, D], F32)
    lrT = carve([16, GW], BF16)
    esp = carve([128, 2, GW], F32)
    csp = carve([128, 2, GW], F32)
    E1 = carve([128, 2, GW], F32)
    E2 = carve([128, 2, GW], F32)
    e1l = [carve([128, 2, GT], F32) for _ in range(2)]
    qf = carve([128, 2, GW], F32)
    kf = carve([128, 2, GW], F32)
    qtl = [carve([128, 2, GW], BF16) for _ in range(2)]
    ktl = [carve([128, 2, GW], BF16) for _ in range(2)]
    khT = carve([128, 2, GW], BF16)
    khtm = [carve([128, GT, 256], BF16) for _ in range(2)]
    Vg = [carve([128, GT, 512], BF16) for _ in range(2)]
    sg = [carve([128, 4, GW], BF16) for _ in range(2)]
    qr = [carve([128, 4, GW], BF16) for _ in range(2)]
    xsb = carve([128, GW], BF16)
    t1 = carve([128, GW], F32)
    t2 = carve([128, GW], F32)
    kS = carve([128, NT + 2, 128], BF16)
    VS = carve([128, NT + 2, 128], BF16)
    un0 = apos[0]
    PT = [carve([128, 3, 512], BF16) for _ in range(2)]
    rden = carve([128, 512], F32)
    un1 = apos[0]
    apos[0] = un0
    gath = carve([128, 3, 514], F32)
    assert apos[0] <= un1
    apos[0] = un1
    AT = carve([128, 512], BF16)
    S = carve([128, 2, 256], F32)
    Sb = carve([128, 2, 256], BF16)
    sq = carve([128, 512], BF16)
    rs = carve([128, 512], F32)
    on = carve([128, 512], F32)
    Sm = on[:, :].rearrange("p (a b) -> p a b", a=2)
    mixT = [carve([128, 8, 128], BF16) for _ in range(2)]
    clsum = carve([128, 2], F32)
    pay = carve([128, 514], F32)
    a1 = carve([128, 2], F32)
    tg = rs[:, :].rearrange("p (a b) -> p a b", a=2)
    ab_end = apos[0]

    apos[0] = 0
    W1b0 = carve([128, 8, FB], BF16)
    W2b0 = carve([128, 8, FB], BF16)
    assert apos[0] <= 8 * NCOL * 2
    fT = carve([128, 8, TOK], BF16)
    W1b = [W1b0, carve([128, 8, FB], BF16)]
    W2b = [W2b0, carve([128, 8, FB], BF16)]
    h1T = [carve([128, 8, 512], BF16) for _ in range(2)]
    rtmp = [carve([128, 512], F32) for _ in range(2)]
    utmC = [carve([128, D], BF16) for _ in range(2)]
    finw = carve([128, D], F32)
    yout = [carve([128, D], F32) for _ in range(2)]
    junkD = [carve([128, D], BF16) for _ in range(2)]

    pT = psum("pT", [128, 1024], BF16)
    pF = [psum(f"pF{i}", [128, 512], F32) for i in range(2)]
    pSC = [psum(f"pSC{i}", [128, 512], F32) for i in range(2)]
    pO = psum("pO", [128, 512], F32)
    pD = psum("pD", [128, 512], F32)
    pW = psum("pW", [128, 512], F32)

    fctr = [0]
    fmode = ['A']

    FBANK = [pF[0], pF[1], pW]

    def nextF():
        i = fctr[0] % (2 if (SPLIT_BACK and fmode[0] == 'B') else 3)
        fctr[0] += 1
        return i

    def Fslot(i, parts=128, width=GW):
        return FBANK[i][0:parts, 0:width]

    sctr = [0]

    def nextSC():
        i = sctr[0] % 2
        sctr[0] += 1
        return i

    def mmchain(out, pairs, reads, writes, start=True, stop=True):
        def fn(e):
            n = len(pairs)
            ins = None
            for idx, (l, r) in enumerate(pairs):
                ins = e.matmul(out=out, lhsT=l, rhs=r, start=(start and idx == 0), stop=(stop and idx == n - 1))
            return ins
        mult = 4 if pairs[0][1].dtype == F32 else 1
        P.op('pe', fn, reads, writes, cost=mult * sum(max(r.free_size(), 64) for (_l, r) in pairs))

    def mmlist(items, reads, writes):
        def fn(e):
            ins = None
            for (o, l, r, s, t) in items:
                ins = e.matmul(out=o, lhsT=l, rhs=r, start=s, stop=t)
            return ins
        P.op('pe', fn, reads, writes, cost=sum(max(it[2].free_size(), 64) for it in items))

    def transposes(items, reads, writes):
        def fn(e):
            ins = None
            for (o, i_) in items:
                ins = e.transpose(out=o, in_=i_, identity=ident)
            return ins
        P.op('pe', fn, list(reads) + ['cstb'], writes, cost=128 * len(items))

    def act(out, in_, func, reads, writes, bias=None, scale=None, accum=None):
        kw = {}
        if bias is not None:
            kw['bias'] = bias
        if scale is not None:
            kw['scale'] = scale
        if accum is not None:
            kw['accum_out'] = accum
        table = 'silu' if func == AF.Silu else ('lnexp' if func in (AF.Exp, AF.Ln) else None)
        P.op('act', lambda e: e.activation(out=out, in_=in_, func=func, **kw), reads, writes,
             dur=0.22 + out.free_size() / 1400.0 + (0.1 if accum is not None else 0.0), table=table)

    def tt(eng, out, in0, in1, op, reads, writes):
        P.op(eng, lambda e: e.tensor_tensor(out=out, in0=in0, in1=in1, op=op), reads, writes,
             dur=0.07 + out.free_size() / 960.0)

    def ts(eng, out, in0, s1, s2, op0, op1, reads, writes):
        if op1 is None:
            P.op(eng, lambda e: e.tensor_scalar(out=out, in0=in0, scalar1=s1, scalar2=None, op0=op0), reads, writes,
                 dur=0.07 + out.free_size() / 1300.0)
        else:
            P.op(eng, lambda e: e.tensor_scalar(out=out, in0=in0, scalar1=s1, scalar2=s2, op0=op0, op1=op1),
                 reads, writes, dur=0.07 + out.free_size() / 1300.0)

    def stt(out, in0, scalar, in1, op0, op1, reads, writes):
        P.op('dve', lambda e: e.scalar_tensor_tensor(out=out, in0=in0, scalar=scalar, in1=in1, op0=op0, op1=op1),
             reads, writes, dur=0.07 + out.free_size() / 960.0)

    def dma(eng, out, in_, sem, reads, writes, extra=(), **kw):
        return P.dma(eng, lambda e: e.dma_start(out=out, in_=in_, **kw), sem, reads, writes, extra=extra)

    dma('sp', cstf, cstf_d, 'c0', [], ['cstf'])
    dma('sp', wgu32, wgu_d, 'c1', [], ['wgu32'])
    dma('sp', sinks_s, sinks_d, 'c2', [], ['sinks'])
    dma('pool', cstb, cstb_d.rearrange("p (a b) -> p a b", a=7), 'c3', [], ['cstb'])
    dma('pool', maskM, maskm_d.rearrange("p (a b) -> p a b", a=2), 'c4', [], ['maskM'])
    win_v = win_d.rearrange("(k p) c -> p k c", p=128)
    dma('pool', win[:, :, C_GK:C_GR], win_v[:, :, C_GK:C_GR], 'w0', [], ['win_gk'])
    i_w1 = dma('pool', win[:, :, C_GV:NCOL], win_v[:, :, C_GV:NCOL], 'w1', [], ['win_tm'])
    dma('sp', xa, xm_d, 'xa0', [], ['xa'])
    i_x = None
    for t in range(NT):
        i_x = dma('sp', h[:, t, :], xo_d[t * 128:(t + 1) * 128, :], f'x{t % 4}', [], [f'h{t}'],
                  extra=[i_w1] if t >= 2 else [])
    dma('pool', win[:, :, C_GQ:C_GK], win_v[:, :, C_GQ:C_GK], 'w2', [], ['win_gq'], extra=[i_x])
    dma('pool', win[:, :, C_GR:C_SQ], win_v[:, :, C_GR:C_SQ], 'w3', [], ['win_gr'], extra=[i_x])
    dma('pool', win[:, :, C_SQ:C_GV], win_v[:, :, C_SQ:C_GV], 'w4', [], ['win_s'], extra=[i_x])
    dma('pool', wout, wout_d.rearrange("(k p) c -> p k c", p=128), 'w5', [], ['wout'], extra=[i_x])
    WIN_ALL = ['win_gk', 'win_tm', 'win_gq', 'win_gr', 'win_s']

    ts('dve', bgn, bgn, -1.0, None, ALU.mult, None, ['cstf'], ['cstf'])
    P.op('dve', lambda e: e.tensor_copy(out=wgu, in_=wgu32), ['wgu32'], ['wgu'])
    P.op('dve', lambda e: e.memset(ones64, 1.0), [], ['ones64'])
    P.op('dve', lambda e: e.memset(epst, EPS), [], ['epst'])
    P.op('dve', lambda e: e.memset(S, 0.0), [], ['S'])
    P.op('dve', lambda e: e.memset(clsum, 0.0), [], ['clsum'])
    for kg in range(2):
        P.op('dve', lambda e, kg=kg: e.tensor_copy(
            out=maskM[0:1, kg, :].rearrange("p (a b) -> p a b", a=4),
            in_=sinks_s[0:1, kg * 4:kg * 4 + 4].unsqueeze(2).broadcast_to([1, 4, 128])),
            ['sinks', 'maskM'], ['maskM'])

    def rms_stats(src, src_key, slot, junk, junk_key):
        ss = stat[:, 4 * slot:4 * slot + 1]
        ln = stat[:, 4 * slot + 1:4 * slot + 2]
        rstd = stat[:, 4 * slot + 2:4 * slot + 3]
        k = f'stat{slot}'
        act(junk, src, AF.Square, [src_key], [junk_key, k], accum=ss)
        act(ln, ss, AF.Ln, [k, 'epst'], [k], bias=epst[:, 0:1], scale=1.0 / D)
        act(rstd, ln, AF.Exp, [k], [k], scale=-0.5)
        return rstd, k

    def norm_transpose(src, src_key, slot, nw, dst, dst_key, utm_bufs, utm_pref):
        u = utm_bufs[slot]
        uk = f'{utm_pref}{slot}'
        rstd, k = rms_stats(src, src_key, slot, u, uk)
        ts('dve', u, src, rstd, None, ALU.mult, None, [src_key, k], [uk])
        transposes([(pT[:, kk * 128:(kk + 1) * 128], u[:, kk * 128:(kk + 1) * 128]) for kk in range(8)],
                   [uk], ['pT'])
        tt('dve', dst, pT[:, :].rearrange("p (a b) -> p a b", a=8),
           nw.unsqueeze(2).broadcast_to([128, 8, 128]), ALU.mult, ['pT', 'cstf'], [dst_key])

    def proj_fm(col0, ncols_m, usl, T, wkey):
        fi = nextF()
        out = Fslot(fi, ncols_m, T)
        mmchain(out, [(win[:, k, col0:col0 + ncols_m], uT[usl][:, k, 0:T]) for k in range(8)],
                [f'uT{usl}.{i}' for i in range((T + 127) // 128)] + [wkey], [f'F{fi}'])
        return out, f'F{fi}'

    def gate_pipeline(usl, T, ntile, need_e1, gs):
        ukeys = [f'uT{usl}.{i}' for i in range(ntile)]
        fi = nextF()
        lo = Fslot(fi, 16, T)
        mmchain(lo, [(win[:, k, C_LR:C_LR + 16], uT[usl][:, k, 0:T]) for k in range(8)],
                ukeys + ['win_tm'], [f'F{fi}'])
        act(lrT[:, 0:T], lo, AF.Copy, [f'F{fi}'], ['lrT'])
        for pc in range(2):
            fz = nextF()
            zo = Fslot(fz, 128, T)
            mmchain(zo, [(wgu[0:16, pc * 128:(pc + 1) * 128], lrT[0:16, 0:T])], ['lrT', 'wgu'], [f'F{fz}'])
            act(esp[:, pc, 0:T], zo, AF.Exp, [f'F{fz}', 'cstf'], [f'esp{pc}'], bias=bgn[:, pc:pc + 1], scale=-1.0)
            act(esp[:, pc, 0:T], esp[:, pc, 0:T], AF.Ln, [f'esp{pc}'], [f'esp{pc}'], bias=1.0)
            P.op('dve', lambda e, pc=pc: e.tensor_tensor_scan(
                out=csp[:, pc, 0:T], data0=resetm[:, 0:T], data1=esp[:, pc, 0:T], initial=0.0,
                op0=ALU.mult, op1=ALU.add), [f'esp{pc}', 'cstf'], [f'csp{pc}'], dur=0.07 + 2 * T / 960.0)
            act(E2[:, pc, 0:T], csp[:, pc, 0:T], AF.Exp, [f'csp{pc}'], [f'E2{pc}'], scale=1.0 / 16)
            if need_e1:
                act(E1[:, pc, 0:T], csp[:, pc, 0:T], AF.Exp, [f'csp{pc}'], [f'E1{pc}'], scale=-1.0 / 16)
        lastc = csp[:, :, 0:T].rearrange("p c (t x) -> p c t x", x=128)[:, :, :, 127]
        act(e1l[gs][:, :, 0:ntile], lastc, AF.Exp, ['csp0', 'csp1'], [f'e1l{gs}'], scale=-1.0 / 16)

    def khat_and_state(ntile, ksrc, kkeys, T, gs):
        for ti in range(ntile):
            cs = slice(ti * 128, (ti + 1) * 128)
            for pc in range(2):
                stt(khT[:, pc, cs], ksrc(pc)[:, cs], e1l[gs][:, pc, ti:ti + 1], E2[:, pc, cs], ALU.mult, ALU.mult,
                    [kkeys(pc), f'e1l{gs}', f'E2{pc}'], [f'khT{pc}.{ti}'])
        for ti in range(ntile):
            cs = slice(ti * 128, (ti + 1) * 128)
            fi = nextF()
            fb_ = FBANK[fi].bitcast(BF16)
            transposes([(fb_[:, pc * 128:(pc + 1) * 128], khT[:, pc, cs]) for pc in range(2)],
                       [f'khT0.{ti}', f'khT1.{ti}'], [f'F{fi}'])
            act(khtm[gs][:, ti, :], fb_[:, 0:256], AF.Copy, [f'F{fi}'], [f'khtm{gs}.{ti}'])

    def state_update(ti, gs, bank=None, bkey=None):
        if bank is None:
            sc = nextSC()
            bank, bkey = pSC[sc], f'SC{sc}'
        mmlist([(bank[:, p * 256:(p + 1) * 256], khtm[gs][:, ti, p * 128:(p + 1) * 128],
                 Vg[gs][:, ti, p * 256:(p + 1) * 256], p == 0, p == 1) for p in range(2)],
               [f'khtm{gs}.{ti}', f'Vg{gs}.{ti}'], [bkey])
        for p in range(2):
            stt(S[:, p, :], S[:, p, :], e1l[gs][:, p, ti:ti + 1], bank[:, p * 256:(p + 1) * 256],
                ALU.mult, ALU.add, ['S', f'e1l{gs}', bkey], ['S'])

    def v_proj(usl, ti, gs):
        fi = nextF()
        bank = FBANK[fi]
        mmchain(bank[:, :], [(uT[usl][:, k, ti * 128:(ti + 1) * 128], win[:, k, C_GV:C_GV + 512]) for k in range(8)],
                [f'uT{usl}.{ti}', 'win_tm'], [f'F{fi}'])
        act(Vg[gs][:, ti, :], bank[:, :], AF.Copy, [f'F{fi}'], [f'Vg{gs}.{ti}'])

    class _Stop(Exception):
        pass

    def rope(ps, pskey, T, tcos, tsin, tkey, scale, out, outkey):
        act(xsb[:, 0:T], ps, AF.Copy, [pskey], ['xsb'])
        if DEBUG_STOP == 'R1':
            raise _Stop(finish_debug_bf(xsb[:, 0:128], 'xsb', 128))
        fr = nextF()
        pr = Fslot(fr, 128, T)
        mmchain(pr, [(Pm, xsb[:, 0:T])], ['xsb', 'cstb'], [f'F{fr}'])
        if DEBUG_STOP == 'R2':
            raise _Stop(finish_debug_bf(pr[:, 0:128], f'F{fr}', 128))
        stt(t1[:, 0:T], ps, scale, tcos, ALU.mult, ALU.mult, [pskey, tkey, 'cstf'] + (['xsb'] if VARX else []), ['t1'])
        if DEBUG_STOP == 'R3':
            raise _Stop(finish_debug(t1[:, 0:128], 't1', 128))
        stt(t2[:, 0:T], pr, scale, tsin, ALU.mult, ALU.mult, [f'F{fr}', tkey, 'cstf'], ['t2'])
        if DEBUG_STOP == 'R4':
            raise _Stop(finish_debug(t2[:, 0:128], 't2', 128))
        tt(PENG, out, t1[:, 0:T], t2[:, 0:T], ALU.add, ['t1', 't2'], [outkey])

    def merge(*lists):
        lists = [l for l in lists if l]
        tot = [sum(o.get('cost', 0) for o in l) or 1 for l in lists]
        idx = [0] * len(lists)
        prog = [0.0] * len(lists)
        out = []
        while True:
            best = None
            for k, l in enumerate(lists):
                if idx[k] < len(l) and (best is None or prog[k] / tot[k] < prog[best] / tot[best]):
                    best = k
            if best is None:
                break
            o = lists[best][idx[best]]
            idx[best] += 1
            prog[best] += o.get('cost', 0)
            out.append(o)
        return out

    esp_s = (esp, E1)
    kf_s = (kf, qf)
    espk = ('esp', 'E1')
    kfk = ('kf', 'qf')
    pObf = pO.bitcast(BF16)

    def Nstage(srcs, usl):
        for ti, (src, skey) in enumerate(srcs):
            norm_transpose(src, skey, ti % 2, nmw, uT[usl][:, :, ti * 128:(ti + 1) * 128], f'uT{usl}.{ti}', utm, 'utm')

    def PjA(T, ntile, usl, gs):
        ukeys = [f'uT{usl}.{i}' for i in range(ntile)]
        fi = nextF()
        lo = Fslot(fi, 16, T)
        mmchain(lo, [(win[:, k, C_LR:C_LR + 16], uT[usl][:, k, 0:T]) for k in range(8)], ukeys + ['win_tm'], [f'F{fi}'])
        act(lrT[:, 0:T], lo, AF.Copy, [f'F{fi}'], ['lrT'])
        for pc in range(2):
            fz = nextF()
            zo = Fslot(fz, 128, T)
            mmchain(zo, [(wgu[0:16, pc * 128:(pc + 1) * 128], lrT[0:16, 0:T])], ['lrT', 'wgu'], [f'F{fz}'])
            act(esp_s[gs][:, pc, 0:T], zo, AF.Exp, [f'F{fz}', 'cstf'], [f'{espk[gs]}{pc}'], bias=bgn[:, pc:pc + 1],
                scale=-1.0)
        for pc in range(2):
            kp, kkey = proj_fm(C_GK + pc * 128, 128, usl, T, 'win_gk')
            act(kf_s[gs][:, pc, 0:T], kp, AF.Copy, [kkey], [f'{kfk[gs]}{pc}'])
        for ti in range(ntile):
            v_proj(usl, ti, gs)

    def A2(T, ntile, gs, own):
        for pc in range(2):
            ek = f'{espk[gs]}{pc}'
            act(esp_s[gs][:, pc, 0:T], esp_s[gs][:, pc, 0:T], AF.Ln, [ek], [ek], bias=1.0)
            P.op('dve', lambda e, pc=pc: e.tensor_tensor_scan(
                out=csp[:, pc, 0:T], data0=resetm[:, 0:T], data1=esp_s[gs][:, pc, 0:T], initial=0.0,
                op0=ALU.mult, op1=ALU.add), [ek, 'cstf'], [f'csp{pc}'], dur=0.07 + 2 * T / 960.0)
            act(E2[:, pc, 0:T], csp[:, pc, 0:T], AF.Exp, [f'csp{pc}'], [f'E2{pc}'], scale=1.0 / 16)
        lastc = csp[:, :, 0:T].rearrange("p c (t x) -> p c t x", x=128)[:, :, :, 127]
        act(e1l[gs][:, :, 0:ntile], lastc, AF.Exp, ['csp0', 'csp1'], [f'e1l{gs}'], scale=-1.0 / 16)
        for ti in range(ntile):
            cs = slice(ti * 128, (ti + 1) * 128)
            for pc in range(2):
                stt(khT[:, pc, cs], kf_s[gs][:, pc, cs], e1l[gs][:, pc, ti:ti + 1], E2[:, pc, cs], ALU.mult, ALU.mult,
                    [f'{kfk[gs]}{pc}', f'e1l{gs}', f'E2{pc}'], [f'khT{pc}.{ti}'])
        for ti in range(ntile):
            cs = slice(ti * 128, (ti + 1) * 128)
            transposes([(pObf[:, pc * 128:(pc + 1) * 128], khT[:, pc, cs]) for pc in range(2)],
                       [f'khT0.{ti}', f'khT1.{ti}'], ['pO0', 'pO1'])
            act(khtm[gs][:, ti, :], pObf[:, 0:256], AF.Copy, ['pO0', 'pO1'], [f'khtm{gs}.{ti}'])
        for ti in range(ntile):
            state_update(ti, gs)
        if own:
            for ti in range(ntile):
                tt('dve', clsum, clsum, lastc[:, :, ti], ALU.add, ['clsum', 'csp0', 'csp1'], ['clsum'])
        else:
            P.op('dve', lambda e: e.tensor_copy(out=Sm, in_=S), ['S'], ['Sm'])
            ts('dve', S[:, :, :], S[:, :, :], flags[:, 3:4], None, ALU.mult, None, ['S', 'cstf'], ['S'])

    GSEQ = [([(xa, 'xa')], 128, 1, False)] + [
        ([(h[:, g * GT + ti, :], f'h{g * GT + ti}') for ti in range(GT)], GW, GT, True) for g in range(NG)]

    def capN(i):
        P.capture()
        Nstage(GSEQ[i][0], i % 2)
        return P.end_capture()

    def capPj(i):
        P.capture()
        PjA(GSEQ[i][1], GSEQ[i][2], i % 2, i % 2)
        return P.end_capture()

    def capA2(i):
        P.capture()
        A2(GSEQ[i][1], GSEQ[i][2], i % 2, GSEQ[i][3])
        return P.end_capture()

    P.play(capN(0))
    P.play_sched([capPj(0), capN(1)])
    for i in range(len(GSEQ)):
        lists = [capA2(i)]
        if i + 1 < len(GSEQ):
            lists.append(capPj(i + 1))
        if i + 2 < len(GSEQ):
            lists.append(capN(i + 2))
        if PIPELINE:
            P.play_sched(lists)
        else:
            for l in lists:
                P.play(l)
    P.op('dve', lambda e: e.tensor_copy(out=pay[:, 0:512], in_=S[:, :, :].rearrange("p a b -> p (a b)")),
         ['S'], ['pay'])
    act(pay[:, 512:514], clsum, AF.Exp, ['clsum', 'pay'], ['pay'], scale=-1.0 / 16)
    dma('pool', cin_d, pay, 'cc0', ['pay'], ['cin'])
    P.dma('pool', lambda e: e.collective_compute("AllGather", ALU.bypass,
                                                  replica_groups=[[0, 1, 2, 3], [4, 5, 6, 7]],
                                                  ins=[cin_d], outs=[cout_d]),
          'cc1', ['cin'], ['cout'], inc=1)
    dma('pool', gath, cout_d[0:3 * 128, :].rearrange("(r p) c -> p r c", p=128), 'cc2', ['cout'], ['gath'])
    def state_chain():
      P.op('dve', lambda e: e.memset(S, 0.0), ['S'], ['S'])
      for j in range(3):
        fj = flags[:, j:j + 1]
        ts('dve', a1, gath[:, j, 512:514], -1.0, fj, ALU.add, ALU.mult, ['gath', 'cstf'], ['a1'])
        ts('dve', a1, a1, 1.0, None, ALU.add, None, ['a1'], ['a1'])
        ts('dve', tg[:, :, :].rearrange("p a b -> p (a b)"), gath[:, j, 0:512], fj, None, ALU.mult, None,
           ['gath', 'cstf'], ['tg'])
        for p in range(2):
            stt(S[:, p, :], S[:, p, :], a1[:, p:p + 1], tg[:, p, :], ALU.mult, ALU.add, ['S', 'a1', 'tg'], ['S'])
      for p in range(2):
        stt(S[:, p, :], Sm[:, p, :], flags[:, 3:4], S[:, p, :], ALU.mult, ALU.add, ['S', 'Sm', 'cstf'], ['S'])
      act(Sb[:, :, :], S[:, :, :], AF.Copy, ['S'], ['Sb'])

    def finish_debug(src_ap, key, width):
        dma('sp', out_d[0:128, 0:width], src_ap, 'dbg', [key], ['dbgout'])
        P._add(dict(eng='sp', fn=None, reads=['dbgout'], writes=[], kind='c', extra=[]))
        P.emit()
        return nc, st

    def finish_debug_bf(src_ap, key, width):
        P.op('dve', lambda e: e.tensor_copy(out=on[:, 0:width], in_=src_ap), [key], ['on'])
        return finish_debug(on[:, 0:width], 'on', width)

    if DEBUG_STOP == 'A0':
        return finish_debug(pay[:, 0:512], 'pay', 512)
    if DEBUG_STOP == 'A':
        state_chain()
        return finish_debug(S[:, :, :].rearrange("p a b -> p (a b)"), 'S', 512)

    for slot in range(2 if not (DEBUG_STOP or '').startswith('R') else 0):
        usl = slot
        if slot == 1:
            dma('sp', xa, xh_d, 'xa0', [], ['xa'])
        norm_transpose(xa, 'xa', slot, nmw, uT[usl][:, :, 0:128], f'uT{usl}.0', utm, 'utm')
        if DEBUG_STOP == f'P{slot}a':
            return finish_debug_bf(uT[usl][:, 0, 0:128], f'uT{usl}.0', 128)
        dma('sp', tabc[slot][:, 0:128], cos_d[:, slot * 128:(slot + 1) * 128], f'tc{slot}', [], [f'tab{slot}'])
        dma('sp', tabs[slot][:, 0:128], sin_d[:, slot * 128:(slot + 1) * 128], f'ts{slot}', [], [f'tab{slot}'])
        kp, kkey = proj_fm(C_SK, 128, usl, 128, 'win_s')
        if DEBUG_STOP == f'P{slot}b':
            return finish_debug_bf(kp, kkey, 128)
        if DEBUG_STOP == f'P{slot}t':
            return finish_debug_bf(tabc[slot][:, 0:128], f'tab{slot}', 128)
        rope(kp, kkey, 128, tabc[slot][:, 0:128], tabs[slot][:, 0:128], f'tab{slot}', c_one, kS[:, slot, :], f'kS{slot}')
        if DEBUG_STOP == f'P{slot}c':
            return finish_debug_bf(kS[:, slot, :], f'kS{slot}', 128)
        fi = nextF()
        vo = Fslot(fi, 128, 128)
        mmchain(vo, [(uT[usl][:, k, 0:128], win[:, k, C_SV:C_SV + 128]) for k in range(8)],
                [f'uT{usl}.0', 'win_tm'], [f'F{fi}'])
        act(VS[:, slot, :], vo, AF.Copy, [f'F{fi}'], [f'VS{slot}'])
        if DEBUG_STOP == f'P{slot}d':
            return finish_debug_bf(VS[:, slot, :], f'VS{slot}', 128)

    if (DEBUG_STOP or '').startswith('R'):
        try:
            slot = usl = 0
            norm_transpose(xa, 'xa', slot, nmw, uT[usl][:, :, 0:128], f'uT{usl}.0', utm, 'utm')
            dma('sp', tabc[slot][:, 0:128], cos_d[:, 0:128], 'tc0', [], ['tab0'])
            dma('sp', tabs[slot][:, 0:128], sin_d[:, 0:128], 'ts0', [], ['tab0'])
            kp, kkey = proj_fm(C_SK, 128, usl, 128, 'win_s')
            rope(kp, kkey, 128, tabc[0][:, 0:128], tabs[0][:, 0:128], 'tab0', c_one, kS[:, 0, :], 'kS0')
        except _Stop as ex:
            return ex.args[0]
    if DEBUG_STOP == 'B0':
        return finish_debug_bf(kS[:, 0:2, :].rearrange("p a b -> p (a b)"), 'kS1', 256)
    if DEBUG_STOP == 'B0v':
        return finish_debug_bf(VS[:, 0:2, :].rearrange("p a b -> p (a b)"), 'VS1', 256)
    def kskey(slot):
        return f'kS{slot}' if slot < 2 else f'kSg{(slot - 2) // GT}'

    def nfront(g):
        usl = g % 2
        for ti in range(GT):
            t = g * GT + ti
            norm_transpose(h[:, t, :], f'h{t}', ti, nmw, uT[usl][:, :, ti * 128:(ti + 1) * 128], f'uT{usl}.{ti}',
                           utm, 'utm')

    def front(g):
        usl = tsl = gs = g % 2
        T = GW
        dma('sp', tabc[tsl], cos_d[:, 256 + g * GW:256 + (g + 1) * GW], f'tc{tsl}', [], [f'tab{tsl}'])
        dma('sp', tabs[tsl], sin_d[:, 256 + g * GW:256 + (g + 1) * GW], f'ts{tsl}', [], [f'tab{tsl}'])
        gate_pipeline(usl, T, GT, True, gs)
        for pc in range(2):
            kp, kkey = proj_fm(C_GK + pc * 128, 128, usl, T, 'win_gk')
            act(kf[:, pc, :], kp, AF.Copy, [kkey], [f'kf{pc}'])
        for pc in range(2):
            qp, qkey = proj_fm(C_GQ + pc * 128, 128, usl, T, 'win_gq')
            act(qf[:, pc, :], qp, AF.Copy, [qkey], [f'qf{pc}'], scale=0.125)
        for pc in range(2):
            tt(PENG, qtl[gs][:, pc, :], qf[:, pc, :], E1[:, pc, :], ALU.mult, [f'qf{pc}', f'E1{pc}'], [f'qtl{gs}.{pc}'])
            tt(PENG, ktl[gs][:, pc, :], kf[:, pc, :], E2[:, pc, :], ALU.mult, [f'kf{pc}', f'E2{pc}'], [f'ktl{gs}.{pc}'])
        khat_and_state(GT, lambda pc: kf[:, pc, :], lambda pc: f'kf{pc}', T, gs)
        for c in range(4):
            rp, rkey = proj_fm(C_GR + c * 128, 128, usl, T, 'win_gr')
            act(sg[gs][:, c, :], rp, AF.Silu, [rkey], [f'sg{gs}.{c}'])
        for c in range(4):
            qp, qkey = proj_fm(C_SQ + c * 128, 128, usl, T, 'win_s')
            rope(qp, qkey, T, tabc[tsl], tabs[tsl], f'tab{tsl}', c_eighth, qr[gs][:, c, :], f'qr{gs}.{c}')
        kp, kkey = proj_fm(C_SK, 128, usl, T, 'win_s')
        rope(kp, kkey, T, tabc[tsl], tabs[tsl], f'tab{tsl}', c_one,
             kS[:, 2 + g * GT:2 + (g + 1) * GT, :].rearrange("p a b -> p (a b)"), f'kSg{g}')
        for ti in range(GT):
            t = g * GT + ti
            v_proj(usl, ti, gs)
            fi = nextF()
            vo = Fslot(fi, 128, 128)
            mmchain(vo, [(uT[usl][:, k, ti * 128:(ti + 1) * 128], win[:, k, C_SV:C_SV + 128]) for k in range(8)],
                    [f'uT{usl}.{ti}', 'win_tm'], [f'F{fi}'])
            act(VS[:, 2 + t, :], vo, AF.Copy, [f'F{fi}'], [f'VS{2 + t}'])

    def wout_tile(t):
        ms = t % 2
        mk = f'mix{ms}'
        for half in range(2):
            if SPLIT_BACK:
                wb, wk = ((pSC[1], 'SC1'), (pW, 'F2'))[half]
            else:
                sc = nextSC()
                wb, wk = pSC[sc], f'SC{sc}'
            mmchain(wb[:, :], [(mixT[ms][:, c, :], wout[:, c, half * 512:(half + 1) * 512]) for c in range(8)],
                    [mk + 's', mk + 'g', 'wout'], [wk])
            tt('dve', h[:, t, half * 512:(half + 1) * 512], wb[:, :], h[:, t, half * 512:(half + 1) * 512],
               ALU.add, [wk, f'h{t}'], [f'h{t}'])

    def swa_tile(t):
        g, ti = t // GT, t % GT
        gs = g % 2
        ms = t % 2
        cs = slice(ti * 128, (ti + 1) * 128)
        mk = f'mix{ms}'
        qkeys = [f'qr{gs}.{j}' for j in range(4)]
        blocks = [(1 + t, maskH if t == 0 else maskP), (2 + t, maskC), (0, None)]
        slots = [b[0] for b in blocks]
        for kg in range(2):
            pb = 64 * kg
            ptb = PT[kg]
            for bi, (slot, msk) in enumerate(blocks):
                sc = 0 if SPLIT_BACK else nextSC()
                if bi < 2:
                    mrhs = msk.unsqueeze(1).broadcast_to([128, 4, 128])
                    mout = pSC[sc][:, :].rearrange("p (a b) -> p a b", a=4)
                else:
                    mrhs = maskM[:, kg, :]
                    mout = pSC[sc][:, :]
                items = [(mout, ident, mrhs, True, False),
                         (pSC[sc][:, :].rearrange("p (a b) -> p a b", a=4), kS[pb:pb + 64, slot, :],
                          qr[gs][pb:pb + 64, :, cs], False, True)]
                mmlist(items, ['cstb', 'maskM', kskey(slot)] + qkeys, [f'SC{sc}'])
                act(ptb[:, bi, :], pSC[sc][:, :], AF.Exp, [f'SC{sc}'], [f'PT{kg}.{bi}'])
        for kg in range(2):
            pb = 64 * kg
            ptb = PT[kg]
            mmlist([(pO[pb:pb + 64, :], VS[:, slots[bi], pb:pb + 64], ptb[:, bi, :], bi == 0, bi == 2)
                    for bi in range(3)],
                   [f'VS{s_}' for s_ in slots] + [f'PT{kg}.{bi}' for bi in range(3)], [f'pO{kg}'])
            mmlist([(pD[pb:pb + 64, :], ones64, ptb[:, bi, :], bi == 0, bi == 2) for bi in range(3)],
                   ['ones64'] + [f'PT{kg}.{bi}' for bi in range(3)], [f'pD{kg}'])
        act(rden, pD[:, :], AF.Ln, ['pD0', 'pD1'], ['rden'])
        act(rden, rden, AF.Exp, ['rden'], ['rden'], scale=-1.0)
        tt('dve', mixT[ms][:, 4:8, :], pO[:, :].rearrange("p (a b) -> p a b", a=4),
           rden[:, :].rearrange("p (a b) -> p a b", a=4), ALU.mult, ['pO0', 'pO1', 'rden'], [mk + 's'])

    def gla_tile(t):
        g, ti = t // GT, t % GT
        gs = g % 2
        ms = t % 2
        cs = slice(ti * 128, (ti + 1) * 128)
        mk = f'mix{ms}'
        qk = [f'qtl{gs}.0', f'qtl{gs}.1']
        kk = [f'ktl{gs}.0', f'ktl{gs}.1']
        if SPLIT_BACK:
            AB = (pSC[1], pW)
            AK = (['SC1'], ['F2'])
            GOB, GOK = AB, AK
        else:
            AB = (pSC[0], pSC[1])
            AK = (['SC0'], ['SC1'])
            GOB = (pO, pD)
            GOK = (['pO0', 'pO1'], ['pD0', 'pD1'])
        items = []
        for hh in range(4):
            p_, r0 = hh // 2, 64 * (hh % 2)
            col = (hh // 2) * 128
            items.append((AB[hh % 2][:, col:col + 128], ktl[gs][r0:r0 + 64, p_, cs], qtl[gs][r0:r0 + 64, p_, cs],
                          hh < 2, hh >= 2))
        mmlist(items, kk + qk, AK[0] + AK[1])
        ATv = AT[:, :].rearrange("p (a b c) -> p a b c", a=2, b=2)
        for par in range(2):
            tt('dve', ATv[:, :, par, :], AB[par][:, 0:256].rearrange("p (a c) -> p a c", a=2),
               causal.unsqueeze(1).broadcast_to([128, 2, 128]), ALU.mult, AK[par] + ['cstb'], ['AT'])
        items = []
        for hh in (0, 2, 1, 3):
            p_, r0 = hh // 2, 64 * (hh % 2)
            col = (hh // 2) * 128
            o_ = GOB[hh % 2][:, col:col + 128]
            items.append((o_, Vg[gs][:, ti, hh * 128:(hh + 1) * 128], AT[:, hh * 128:(hh + 1) * 128],
                          hh // 2 == 0, False))
            items.append((o_, Sb[r0:r0 + 64, p_, (hh % 2) * 128:(hh % 2) * 128 + 128], qtl[gs][r0:r0 + 64, p_, cs],
                          False, hh // 2 == 1))
        mmlist(items, [f'Vg{gs}.{ti}', 'AT', 'Sb'] + qk, GOK[0] + GOK[1])
        sqv = sq[:, :].rearrange("p (a b c) -> p a b c", a=2, b=2)
        onv = on[:, :].rearrange("p (a b c) -> p a b c", a=2, b=2)
        rsv = rs[:, :].rearrange("p (a b c) -> p a b c", a=2, b=2)
        if SPLIT_BACK:
            for par in range(2):
                src = GOB[par][:, 0:256].rearrange("p (a c) -> p a c", a=2)
                act(sqv[:, :, par, :], src, AF.Square, GOK[par], ['sq'])
                ts('dve', onv[:, :, par, :], src, gnw, None, ALU.mult, None, GOK[par] + ['cstf'], ['on'])
            state_update(ti, gs, GOB[0], GOK[0][0])
            act(Sb[:, :, :], S[:, :, :], AF.Copy, ['S'], ['Sb'])
            mmchain(GOB[1][:, :], [(onesdiv, sq)], ['sq', 'cstb'], GOK[1])
            act(rs, GOB[1][:, :], AF.Ln, GOK[1] + ['epst'], ['rs'], bias=epst[:, 0:1])
            act(rs, rs, AF.Exp, ['rs'], ['rs'], scale=-0.5)
            tt('dve', on, on, rs, ALU.mult, ['on', 'rs'], ['on'])
        else:
            state_update(ti, gs)
            act(Sb[:, :, :], S[:, :, :], AF.Copy, ['S'], ['Sb'])
            for par in range(2):
                act(sqv[:, :, par, :], GOB[par][:, 0:256].rearrange("p (a c) -> p a c", a=2), AF.Square,
                    GOK[par], ['sq'])
            sc = nextSC()
            mmchain(pSC[sc][:, :], [(onesdiv, sq)], ['sq', 'cstb'], [f'SC{sc}'])
            act(rs, pSC[sc][:, :], AF.Ln, [f'SC{sc}', 'epst'], ['rs'], bias=epst[:, 0:1])
            act(rs, rs, AF.Exp, ['rs'], ['rs'], scale=-0.5)
            for par in range(2):
                stt(onv[:, :, par, :], GOB[par][:, 0:256].rearrange("p (a c) -> p a c", a=2), gnw, rsv[:, :, par, :],
                    ALU.mult, ALU.mult, GOK[par] + ['cstf', 'rs'], ['on'])
        tt(PENG, mixT[ms][:, 0:4, :], on[:, :].rearrange("p (a b) -> p a b", a=4), sg[gs][:, :, cs], ALU.mult,
           ['on'] + [f'sg{gs}.{c}' for c in range(4)], [mk + 'g'])

    def back(g):
        for ti in range(GT):
            t = g * GT + ti
            swa_tile(t)
            if t > 0:
                wout_tile(t - 1)
            gla_tile(t)

    def back_swa(g):
        for ti in range(GT):
            swa_tile(g * GT + ti)

    def back_gla(g):
        for ti in range(GT):
            t = g * GT + ti
            if t > 0:
                wout_tile(t - 1)
            gla_tile(t)

    def cap(fn, *a):
        P.capture()
        fn(*a)
        return P.end_capture()

    fmode[0] = 'B'
    P.play(cap(nfront, 0))
    P.play_sched([cap(front, 0), cap(nfront, 1)])
    state_chain()
    for g in range(NG):
        lists = [cap(back_swa, g), cap(back_gla, g)] if SPLIT_BACK else [cap(back, g)]
        if g + 1 < NG:
            lists.append(cap(front, g + 1))
        if g + 2 < NG:
            lists.append(cap(nfront, g + 2))
        if PIPELINE:
            P.play_sched(lists)
        else:
            for l in lists:
                P.play(l)
    wout_tile(NT - 1)

    if DEBUG_STOP == 'B':
        return finish_debug(h[:, NT - 1, :], f'h{NT - 1}', 1024)
    fmode[0] = 'C'
    w1_v = w1_d.rearrange("(k p) f -> p k f", p=128)

    def load_w1(fb, extra=()):
        s_ = fb % 2
        P.dma('pool', lambda e: e.dma_start(out=W1b[s_], in_=w1_v[:, :, fb * FB:(fb + 1) * FB]), f'f1{s_}', [], [f'W1{s_}'],
              extra=extra, dur=40.0)

    def load_w2(fb, extra=()):
        s_ = fb % 2
        P.dma('pool', lambda e: e.dma_start(out=W2b[s_], in_=w2_d[fb * FB:(fb + 1) * FB, :].rearrange("(c p) d -> p c d", p=128)),
              f'f2{s_}', [], [f'W2{s_}'], extra=extra, dur=40.0)

    load_w1(0, extra=[P.last_win])
    load_w2(0, extra=[P.last_win])

    lastB = [P.last_on[e] for e in ('pe', 'act', 'dve', 'pool', 'sp') if e in P.last_on]
    for e in ('pe', 'act', 'dve', 'pool', 'sp'):
        P._add(dict(eng=e, fn=None, reads=[], writes=[], kind='c', extra=list(lastB)))

    load_w1(1)
    load_w2(1)
    dma('sp', finw, finw_d, 'fw', [], ['finw'])

    def make_fT(t):
        norm_transpose(h[:, t, :], f'h{t}', t % 2, nfw, fT[:, :, t * 128:(t + 1) * 128], f'fT{t}', utmC, 'utmC')

    cctr = [0]

    def ffn1(u):
        fb, g4 = divmod(u, 4)
        s_, hs = fb % 2, u % 2
        tcs = slice(g4 * 512, (g4 + 1) * 512)
        for fc in range(8):
            bi = cctr[0] % 2
            cctr[0] += 1
            bank = pF[bi]
            mmchain(bank[:, :], [(W1b[s_][:, k, fc * 128:(fc + 1) * 128], fT[:, k, tcs]) for k in range(8)],
                    [f'W1{s_}'] + [f'fT{g4 * 4 + i}' for i in range(4)], [f'F{bi}'])
            act(rtmp[bi], bank[:, :], AF.Relu, [f'F{bi}'], [f'rt{bi}'])
            tt(PENG, h1T[hs][:, fc, :], rtmp[bi], rtmp[bi], ALU.mult, [f'rt{bi}'], [f'h1T{hs}.{fc}'])
        if g4 == 3 and fb + 2 < NFB:
            load_w1(fb + 2)

    def final_tile(t):
        s_ = t % 2
        rstd, k = rms_stats(h[:, t, :], f'h{t}', 2 + s_, junkD[s_], f'junkD{s_}')
        stt(yout[s_], h[:, t, :], rstd, finw, ALU.mult, ALU.mult, [f'h{t}', k, 'finw'], [f'y{s_}'])
        dma('sp', out_d[t * 128:(t + 1) * 128, :], yout[s_], f'o{s_}', [f'y{s_}'], [f'out{t}'])

    def ffn2(u):
        fb, g4 = divmod(u, 4)
        s_, hs = fb % 2, u % 2
        for ti in range(4):
            t = g4 * 4 + ti
            for half in range(2):
                ob = (pSC[0], pSC[1], pO, pD)[(ti % 2) * 2 + half]
                okey = ('SC0', 'SC1', 'pO0', 'pD0')[(ti % 2) * 2 + half]
                mmchain(ob[:, :], [(h1T[hs][:, fc, ti * 128:(ti + 1) * 128], W2b[s_][:, fc, half * 512:(half + 1) * 512])
                                   for fc in range(8)],
                        [f'W2{s_}'] + [f'h1T{hs}.{fc}' for fc in range(8)], [okey])
                tt('dve', h[:, t, half * 512:(half + 1) * 512], ob[:, :], h[:, t, half * 512:(half + 1) * 512],
                   ALU.add, [okey, f'h{t}'], [f'h{t}'])
            if fb == NFB - 1:
                final_tile(t)
        if g4 == 3 and fb + 2 < NFB:
            load_w2(fb + 2)

    for t in range(4):
        make_fT(t)
    NU = NFB * 4

    def ffn_stream():
        ffn1(0)
        for u in range(NU):
            if u + 1 < NU:
                ffn1(u + 1)
            ffn2(u)

    def fT_rest():
        for t in range(4, NT):
            make_fT(t)

    P.play_sched([cap(ffn_stream), cap(fT_rest)])
    P._add(dict(eng='sp', fn=None, reads=[f'out{t}' for t in range(NT)], writes=[], kind='c', extra=[]))

    global LAST_PROG
    LAST_PROG = P
    P.emit()
    return nc, st


def _host_constants(seg):
    f32 = np.float32
    jj = np.arange(128)[:, None]
    rr = np.arange(128)[None, :]
    ident = np.eye(128, dtype=f32)
    maskP = np.where(jj > rr, 0.0, NEG).astype(f32)
    maskC = np.where(jj <= rr, 0.0, NEG).astype(f32)
    maskH = maskP if seg > 0 else np.full((128, 128), NEG, f32)
    causal = (jj <= rr).astype(f32)
    onesdiv = np.full((128, 128), 1.0 / 128, f32)
    Pm = np.zeros((128, 128), f32)
    for m in range(128):
        dm = m % 64
        if dm < 8:
            Pm[m + 8, m] = 1.0
        elif dm < 16:
            Pm[m - 8, m] = 1.0
    cstb = np.concatenate([ident, maskP, maskC, maskH, causal, onesdiv, Pm], axis=1)
    maskm = np.zeros((128, 1024), f32)
    maskm[1:112, :] = NEG
    inv_freq = (1.0 / (f32(ROPE_THETA) ** (np.arange(0, 16, 2, dtype=f32) / f32(16)))).astype(f32)
    t0 = seg * TOK
    pos = np.zeros(2304, np.int64)
    pos[112:128] = np.arange(16)
    pos[128:256] = np.maximum(N_META + t0 - 128 + np.arange(128), 0)
    pos[256:] = N_META + t0 + np.arange(TOK)
    ang = (pos.astype(f32)[None, :] * inv_freq[:, None]).astype(f32)
    cosv = np.cos(ang.astype(np.float64)).astype(f32)
    sinv = np.sin(ang.astype(np.float64)).astype(f32)
    cosT = np.ones((128, 2304), f32)
    sinT = np.zeros((128, 2304), f32)
    for p in range(128):
        dm = p % 64
        if dm < 16:
            cosT[p] = cosv[dm % 8]
            sinT[p] = -sinv[dm % 8] if dm < 8 else sinv[dm % 8]
    return cstb, maskm, cosT, sinT


_CACHE = {}


def kernel(x, meta_tokens, norm_mix_w, w_in, w_gate_up, b_gate, gla_norm_w, sinks, w_out, norm_ff_w,
           w_ff1, w_ff2, final_norm_w):
    f32 = np.float32
    x = np.asarray(x, f32)
    meta = np.asarray(meta_tokens, f32)
    w_in0 = np.asarray(w_in, f32)[0]
    w_out0 = np.asarray(w_out, f32)[0]
    o_gq, o_gk, o_gv, o_gr, o_lr, o_sq, o_sk, o_sv = 0, 256, 512, 1024, 1536, 1552, 2064, 2192
    sq_cols = []
    for j in range(4):
        for hd in (j, 4 + j):
            sq_cols.extend(range(o_sq + hd * 64, o_sq + hd * 64 + 64))
    cols = (list(range(o_gq, o_gq + 256)) + list(range(o_gk, o_gk + 256)) + list(range(o_gr, o_gr + 512))
            + sq_cols + list(range(o_sk, o_sk + 128)) + list(range(o_gv, o_gv + 512))
            + list(range(o_sv, o_sv + 128)) + list(range(o_lr, o_lr + 16)))
    assert len(cols) == NCOL
    w_in_p = np.ascontiguousarray(w_in0[:, cols])
    rows = list(range(512))
    for j in range(4):
        for hd in (j, 4 + j):
            rows.extend(range(512 + hd * 64, 512 + hd * 64 + 64))
    w_out_p = np.ascontiguousarray(w_out0[rows, :])

    cstf_base = np.zeros((128, CF_N), f32)
    cstf_base[:, CF_RESET:CF_RESET + 256] = 1.0
    cstf_base[:, CF_RESET + 0] = 0.0
    cstf_base[:, CF_RESET + 128] = 0.0
    cstf_base[:, CF_NMW:CF_NMW + 8] = np.asarray(norm_mix_w, f32)[0].reshape(8, 128).T
    cstf_base[:, CF_NFW:CF_NFW + 8] = np.asarray(norm_ff_w, f32)[0].reshape(8, 128).T
    cstf_base[:, CF_ONE] = 1.0
    cstf_base[:, CF_EIGHTH] = 0.125
    cstf_base[:, CF_GNW] = np.asarray(gla_norm_w, f32)[0]
    cstf_base[:, CF_BGN:CF_BGN + 2] = np.asarray(b_gate, f32)[0].reshape(2, 128).T
    finw = np.ascontiguousarray(np.broadcast_to(np.asarray(final_norm_w, f32)[None, :], (128, D)))
    xm = np.zeros((128, D), f32)
    xm[112:128] = meta

    in_maps = []
    for c in range(NCORES):
        b, s = c // NSEG, c % NSEG
        t0 = s * TOK
        cstb, maskm, cosT, sinT = _host_constants(s)
        cstf = cstf_base.copy()
        for j in range(3):
            cstf[:, CF_FLAGS + j] = 1.0 if j < s else 0.0
        cstf[:, CF_FLAGS + 3] = 1.0 if s == 0 else 0.0
        xh = x[b, t0 - 128:t0] if s > 0 else np.zeros((128, D), f32)
        in_maps.append({
            "xo": np.ascontiguousarray(x[b, t0:t0 + TOK]),
            "xh": np.ascontiguousarray(xh),
            "xm": xm,
            "w_in": w_in_p,
            "w_out": w_out_p,
            "w_ff1": np.ascontiguousarray(np.asarray(w_ff1, f32)[0]),
            "w_ff2": np.ascontiguousarray(np.asarray(w_ff2, f32)[0]),
            "wgu": np.ascontiguousarray(np.asarray(w_gate_up, f32)[0]),
            "cstb": cstb,
            "maskm": maskm,
            "cstf": cstf,
            "finw": finw,
            "sinks": np.ascontiguousarray(np.asarray(sinks, f32).reshape(1, 8)),
            "cosT": cosT,
            "sinT": sinT,
        })
    if 'nc' not in _CACHE:
        _CACHE['nc'] = build_program()
    nc, _st = _CACHE['nc']
    res = run_bass_kernel_spmd(nc, in_maps, core_ids=list(range(NCORES)))
    out = np.zeros((BATCH, SEQ, D), f32)
    for c in range(NCORES):
        b, s = c // NSEG, c % NSEG
        out[b, s * TOK:(s + 1) * TOK] = res.results[c]["out"]
    return out
```
